# Optimizing a Trainium2 kernel written in Bass

```python
import math
import jax, jax.numpy as jnp
from jax import lax
import numpy as np

D_MODEL = 1024
BATCH = 8
SEQ = 4096
DEPTH = 2

N_MIXERS = 2
D_FF = 2816
NORM_EPS = 1e-6

GLA_HEADS = 4
GLA_DK = D_MODEL // 2
GLA_DV = D_MODEL
GLA_HEAD_K = GLA_DK // GLA_HEADS
GLA_HEAD_V = GLA_DV // GLA_HEADS
GLA_GATE_RANK = 16
GLA_TAU = 16.0
GLA_CHUNK = 64
GLA_IN = 2 * GLA_DK + 2 * GLA_DV + GLA_GATE_RANK

DIFF_HEADS = 8
DIFF_HEAD_DIM = D_MODEL // (2 * DIFF_HEADS)
DIFF_V_DIM = 2 * DIFF_HEAD_DIM
DIFF_IN = 3 * D_MODEL
Q_BLOCK = 128

kernel_name = "hybrid_gla_diffattn_macaron"


def rms_norm(x, g):
    xf = x.astype(jnp.float32)
    y = xf * lax.rsqrt(jnp.mean(xf * xf, axis=-1, keepdims=True) + NORM_EPS)
    return (y * g.astype(jnp.float32)).astype(x.dtype)


def swiglu_ffn(h, w_in, w_down):
    gate, up = jnp.split(h @ w_in, 2, axis=-1)
    return (jax.nn.silu(gate) * up) @ w_down


def lambda_init_fn(layer_idx):
    return 0.8 - 0.6 * math.exp(-0.3 * layer_idx)


def gla_mixer(h, w_in, w_gate2, b_gate2, g_out, w_out):
    B, T, _ = h.shape
    n_chunks = T // GLA_CHUNK
    f32 = jnp.float32
    q, k, v, r, g_lr = jnp.split(h @ w_in, [GLA_DK, 2 * GLA_DK, 2 * GLA_DK + GLA_DV, 2 * GLA_DK + 2 * GLA_DV], axis=-1)
    log_a = jax.nn.log_sigmoid((g_lr @ w_gate2 + b_gate2).astype(f32)) / GLA_TAU

    def heads(t, d):
        return t.reshape(B, n_chunks, GLA_CHUNK, GLA_HEADS, d).transpose(0, 3, 1, 2, 4)

    q = heads(q, GLA_HEAD_K).astype(f32) * (GLA_HEAD_K ** -0.5)
    k = heads(k, GLA_HEAD_K).astype(f32)
    v = heads(v, GLA_HEAD_V).astype(f32)
    b = jnp.cumsum(heads(log_a, GLA_HEAD_K), axis=3)
    b_last = b[:, :, :, -1:, :]
    q_dec = q * jnp.exp(b)
    k_inv = k * jnp.exp(-b)
    k_tail = k * jnp.exp(b_last - b)

    causal = jnp.tril(jnp.ones((GLA_CHUNK, GLA_CHUNK), dtype=bool))
    att = jnp.where(causal, jnp.einsum('bhncd,bhnsd->bhncs', q_dec, k_inv), 0.0)
    o_intra = jnp.einsum('bhncs,bhnsv->bhncv', att, v)

    u = jnp.einsum('bhncd,bhncv->bhndv', k_tail, v)
    decay = jnp.exp(b_last[:, :, :, 0, :])

    def step(state, inp):
        dec, inc = inp
        return dec[..., None] * state + inc, state

    s0 = jnp.zeros((B, GLA_HEADS, GLA_HEAD_K, GLA_HEAD_V), f32)
    _, s_prev = lax.scan(step, s0, (jnp.moveaxis(decay, 2, 0), jnp.moveaxis(u, 2, 0)))
    s_prev = jnp.moveaxis(s_prev, 0, 2)
    o = o_intra + jnp.einsum('bhncd,bhndv->bhncv', q_dec, s_prev)

    o = rms_norm(o, g_out)
    o = o.transpose(0, 2, 3, 1, 4).reshape(B, T, GLA_DV).astype(h.dtype)
    return (o * jax.nn.silu(r)) @ w_out


def diff_mixer(h, w_in, lam_q1, lam_k1, lam_q2, lam_k2, g_out, w_out, lambda_init):
    B, T, _ = h.shape
    f32 = jnp.float32
    q, k, v = jnp.split(h @ w_in, 3, axis=-1)
    q = q.reshape(B, T, DIFF_HEADS, 2, DIFF_HEAD_DIM)
    k = k.reshape(B, T, DIFF_HEADS, 2, DIFF_HEAD_DIM)
    v = v.reshape(B, T, DIFF_HEADS, DIFF_V_DIM)
    lam = (jnp.exp(jnp.sum(lam_q1.astype(f32) * lam_k1.astype(f32)))
           - jnp.exp(jnp.sum(lam_q2.astype(f32) * lam_k2.astype(f32))) + lambda_init)
    scale = DIFF_HEAD_DIM ** -0.5

    outs = []
    for i in range(T // Q_BLOCK):
        q_end = (i + 1) * Q_BLOCK
        qb = q[:, i * Q_BLOCK:q_end]
        kb = k[:, :q_end]
        vb = v[:, :q_end]
        s = jnp.einsum('bqhcd,bkhcd->bhcqk', qb, kb).astype(f32) * scale
        q_pos = i * Q_BLOCK + jnp.arange(Q_BLOCK)
        k_pos = jnp.arange(q_end)
        s = jnp.where(q_pos[:, None] >= k_pos[None, :], s, -jnp.inf)
        p = jax.nn.softmax(s, axis=-1)
        pd = p[:, :, 0] - lam * p[:, :, 1]
        outs.append(jnp.einsum('bhqk,bkhv->bqhv', pd.astype(vb.dtype), vb))
    o = jnp.concatenate(outs, axis=1)

    o = rms_norm(o, g_out) * (1.0 - lambda_init)
    return o.reshape(B, T, D_MODEL) @ w_out


def setup_inputs(seed: int = 0) -> dict:
    key = jax.random.key(seed)
    ks = iter(jax.random.split(key, 64))

    def w(shape, fan_in):
        return jax.random.normal(next(ks), shape, jnp.float32) * (fan_in ** -0.5)

    def gain(n):
        return 1.0 + 0.02 * jax.random.normal(next(ks), (n,), jnp.float32)

    def small(shape, s):
        return s * jax.random.normal(next(ks), shape, jnp.float32)

    inp = {"x": jax.random.normal(next(ks), (BATCH, SEQ, D_MODEL), jnp.float32)}
    inp["l0_norm_ffn1"] = gain(D_MODEL)
    inp["l0_ffn1_w_in"] = w((D_MODEL, 2 * D_FF), D_MODEL)
    inp["l0_ffn1_w_down"] = w((D_FF, D_MODEL), D_FF)
    inp["l0_norm_mix"] = gain(D_MODEL)
    inp["l0_gla_w_in"] = w((D_MODEL, GLA_IN), D_MODEL)
    inp["l0_gla_w_gate2"] = w((GLA_GATE_RANK, GLA_DK), GLA_GATE_RANK)
    inp["l0_gla_b_gate2"] = small((GLA_DK,), 0.1)
    inp["l0_gla_norm_out"] = gain(GLA_HEAD_V)
    inp["l0_gla_w_out"] = w((GLA_DV, D_MODEL), GLA_DV)
    inp["l0_norm_ffn2"] = gain(D_MODEL)
    inp["l0_ffn2_w_in"] = w((D_MODEL, 2 * D_FF), D_MODEL)
    inp["l0_ffn2_w_down"] = w((D_FF, D_MODEL), D_FF)
    inp["l1_norm_ffn1"] = gain(D_MODEL)
    inp["l1_ffn1_w_in"] = w((D_MODEL, 2 * D_FF), D_MODEL)
    inp["l1_ffn1_w_down"] = w((D_FF, D_MODEL), D_FF)
    inp["l1_norm_mix"] = gain(D_MODEL)
    inp["l1_diff_w_in"] = w((D_MODEL, DIFF_IN), D_MODEL)
    inp["l1_diff_lambda_q1"] = small((DIFF_HEAD_DIM,), 0.1)
    inp["l1_diff_lambda_k1"] = small((DIFF_HEAD_DIM,), 0.1)
    inp["l1_diff_lambda_q2"] = small((DIFF_HEAD_DIM,), 0.1)
    inp["l1_diff_lambda_k2"] = small((DIFF_HEAD_DIM,), 0.1)
    inp["l1_diff_norm_out"] = gain(DIFF_V_DIM)
    inp["l1_diff_w_out"] = w((D_MODEL, D_MODEL), D_MODEL)
    inp["l1_norm_ffn2"] = gain(D_MODEL)
    inp["l1_ffn2_w_in"] = w((D_MODEL, 2 * D_FF), D_MODEL)
    inp["l1_ffn2_w_down"] = w((D_FF, D_MODEL), D_FF)
    inp["final_norm"] = gain(D_MODEL)
    return inp


def reference(x,
              l0_norm_ffn1, l0_ffn1_w_in, l0_ffn1_w_down,
              l0_norm_mix, l0_gla_w_in, l0_gla_w_gate2, l0_gla_b_gate2, l0_gla_norm_out, l0_gla_w_out,
              l0_norm_ffn2, l0_ffn2_w_in, l0_ffn2_w_down,
              l1_norm_ffn1, l1_ffn1_w_in, l1_ffn1_w_down,
              l1_norm_mix, l1_diff_w_in, l1_diff_lambda_q1, l1_diff_lambda_k1, l1_diff_lambda_q2,
              l1_diff_lambda_k2, l1_diff_norm_out, l1_diff_w_out,
              l1_norm_ffn2, l1_ffn2_w_in, l1_ffn2_w_down,
              final_norm):
    ffn_pre = [(l0_norm_ffn1, l0_ffn1_w_in, l0_ffn1_w_down),
               (l1_norm_ffn1, l1_ffn1_w_in, l1_ffn1_w_down)]
    ffn_post = [(l0_norm_ffn2, l0_ffn2_w_in, l0_ffn2_w_down),
                (l1_norm_ffn2, l1_ffn2_w_in, l1_ffn2_w_down)]
    mix_norms = [l0_norm_mix, l1_norm_mix]
    mixers = [
        lambda hh: gla_mixer(hh, l0_gla_w_in, l0_gla_w_gate2, l0_gla_b_gate2, l0_gla_norm_out, l0_gla_w_out),
        lambda hh: diff_mixer(hh, l1_diff_w_in, l1_diff_lambda_q1, l1_diff_lambda_k1, l1_diff_lambda_q2,
                              l1_diff_lambda_k2, l1_diff_norm_out, l1_diff_w_out, lambda_init_fn(1)),
    ]
    for i in range(DEPTH):
        g, w_in, w_down = ffn_pre[i]
        x = x + 0.5 * swiglu_ffn(rms_norm(x, g), w_in, w_down)
        x = x + mixers[i](rms_norm(x, mix_norms[i]))
        g, w_in, w_down = ffn_post[i]
        x = x + 0.5 * swiglu_ffn(rms_norm(x, g), w_in, w_down)
    return rms_norm(x, final_norm)
```

```python
import numpy as np
import ml_dtypes
from contextlib import ExitStack
import concourse.bass as bass
import concourse.mybir as mybir
from concourse.bass_utils import run_bass_kernel_spmd

F32 = mybir.dt.float32
BF16 = mybir.dt.bfloat16
ALU = mybir.AluOpType
AF = mybir.ActivationFunctionType

D = 1024
SEQ = 4096
DFF = 2816
NHC = DFF // 128
KC = D // 128
TT = 1024
NS = TT // 128
NT = SEQ // TT
EPS = 1e-6
GLA_DK, GLA_DV, GLA_R = 512, 1024, 16
GLA_IN = 2 * GLA_DK + 2 * GLA_DV + GLA_R
LAMBDA_INIT = 0.8 - 0.6 * float(np.exp(-0.3))
SEM_EPOCH = 30000


class Eng:
    def __init__(self, k, name, h, ordered=False):
        self.k, self.name, self.h, self.ordered = k, name, h, ordered
        self.epoch = -1
        self.count = 0
        self.known = {}
        self.hist = {}
        self.new_epoch()

    def new_epoch(self):
        self.epoch += 1
        self.sem = self.k.new_sem(f"{self.name}{self.epoch}")
        self.key = (self.name, self.epoch)
        self.k.semof[self.key] = self.sem
        self.count = 0
        self.pending = False


class Tl:
    def __init__(self, name, dsem=None):
        self.name = name
        self.w = None
        self.r = []
        self.dsem = dsem
        self.dcount = 0


class K:
    def __init__(self, nc, stack):
        self.nc, self.stack = nc, stack
        self.semof = {}
        self.nsem = 0
        self.pe = Eng(self, "pe", nc.tensor, ordered=True)
        self.act = Eng(self, "act", nc.scalar)
        self.dve = Eng(self, "dve", nc.vector)
        self.pool = Eng(self, "pool", nc.gpsimd)
        self.sp = Eng(self, "sp", nc.sync)
        self.nwaits = 0
        self.ninst = 0
        self.dcnt = {}
        self.live = []

    def new_sem(self, name):
        self.nsem += 1
        return self.stack.enter_context(self.nc.semaphore(f"s_{name}_{self.nsem}"))

    def dsem(self, name):
        key = ("dma", name, self.nsem)
        self.semof[key] = self.new_sem("d" + name)
        self.dcnt[key] = 0
        return key

    def tile(self, name, dma=False):
        t = Tl(name)
        if dma:
            t.dsem = self.dsem(name)
        return t

    def view(self, name, lo, hi, dsem=None):
        t = Tl(name, dsem)
        t.lo, t.hi = lo, hi
        pend = {}
        for o in list(self.live):
            if o.lo < hi and lo < o.hi:
                for d in ([o.w] if o.w else []) + o.r:
                    if pend.get(d[0], 0) < d[1]:
                        pend[d[0]] = d[1]
                self.live.remove(o)
                o.dead = True
        t.r = list(pend.items())
        self.live.append(t)
        return t

    def _deps(self, eng, reads, writes):
        deps = {}
        for t in list(reads) + list(writes):
            if getattr(t, "dead", False):
                raise RuntimeError(f"use of retired view {t.name}")
        def add(d):
            if d is None:
                return
            key, c = d
            if deps.get(key, 0) < c:
                deps[key] = c
        for t in reads:
            add(t.w)
        for t in writes:
            add(t.w)
            for d in t.r:
                add(d)
        return deps

    def _wait(self, eng, deps):
        for key, c in deps.items():
            if eng.ordered and key[0] == eng.name:
                continue
            if eng.known.get(key, 0) >= c:
                continue
            if key[0] != "dma":
                src = getattr(self, key[0])
                if key == src.key and c > src.count:
                    raise RuntimeError(f"wait on unissued inc: {eng.name} waits {key} {c} > {src.count}")
            eng.h.wait_ge(self.semof[key], c)
            self.nwaits += 1
            eng.known[key] = c
            hk = self.hist_lookup(key, c)
            if hk:
                for k2, c2 in hk.items():
                    if eng.known.get(k2, 0) < c2:
                        eng.known[k2] = c2

    def hist_lookup(self, key, c):
        if key[0] == "dma":
            return None
        src = getattr(self, key[0])
        return src.hist.get((key, c))

    def op(self, eng, fn, reads=(), writes=(), inc=True):
        deps = self._deps(eng, reads, writes)
        self._wait(eng, deps)
        ins = fn()
        self.ninst += 1
        if inc:
            if eng.count >= SEM_EPOCH and not eng.pending:
                eng.new_epoch()
            eng.count += 1
            ins.then_inc(eng.sem, 1)
            my = (eng.key, eng.count)
            eng.hist[my] = dict(eng.known)
            eng.pending = False
        else:
            my = (eng.key, eng.count + 1)
            eng.pending = True
        for t in reads:
            t.r.append(my)
        for t in writes:
            t.w = my
            t.r = []
        return ins

    def dma(self, q, dst, src, pairs):
        assert dst.dsem is not None
        srcs = [] if src is None else (list(src) if isinstance(src, (list, tuple)) else [src])
        deps = self._deps(q, srcs, [dst])
        self._wait(q, deps)
        sem = self.semof[dst.dsem]
        for o, i in pairs:
            q.h.dma_start(out=o, in_=i).then_inc(sem, 16)
            self.dcnt[dst.dsem] += 16
        my = (dst.dsem, self.dcnt[dst.dsem])
        for t in srcs:
            t.r.append(my)
        dst.w = my
        dst.r = []

    def wait_tile(self, eng, t):
        self._wait(eng, self._deps(eng, [t], []))


A_BYTES = NHC * TT * 2
W_BYTES = NHC * D * 2
B_BYTES = 32768
WORK_BYTES = A_BYTES + W_BYTES + B_BYTES
NEG = -30000.0


def build_nc(stages=99, ntiles=NT):
    nc = bass.Bass("TRN2", target_bir_lowering=False)
    stack = ExitStack()
    k = K(nc, stack)
    PE, ACT, DVE, POOL, SP = k.pe, k.act, k.dve, k.pool, k.sp

    def din(name, shape, dt=F32):
        return nc.dram_tensor(name, list(shape), dt, kind="ExternalInput").ap()

    x_d = din("x", [SEQ, D])
    out_d = nc.dram_tensor("out", [SEQ, D], F32, kind="ExternalOutput").ap()
    ident_d = din("ident", [128, 128], BF16)
    norms_d = din("norms", [128, 7, KC])
    ffn_w_in = [din(f"ffn{i}_w_in", [D, 2 * DFF]) for i in range(4)]
    ffn_w_down = [din(f"ffn{i}_w_down", [DFF, D]) for i in range(4)]
    gla_w_in_d = din("gla_w_in", [D, GLA_IN])
    gla_wg2b_d = din("gla_wg2b", [17, GLA_DK])
    gla_gout_d = din("gla_gout", [128, 256])
    gla_w_out_d = din("gla_w_out", [D, D])
    diff_w_in_d = din("diff_w_in", [D, 3 * D])
    diff_lam_d = din("diff_lam", [128, 4, 64])
    diff_gout_d = din("diff_gout", [128, 128])
    diff_w_out_d = din("diff_w_out", [D, D])
    tri_d = din("tri", [128, 128])
    maskT_d = din("maskT", [128, 128])
    negmask_d = din("negmask", [128, 128], BF16)
    gfin_d = din("gfin", [128, D])
    kcache = nc.dram_tensor("kcache", [8, 128, SEQ], BF16).ap()
    vcache = nc.dram_tensor("vcache", [8, 128, SEQ // 128, 130], BF16).ap()

    def sb(name, shape, dt):
        return stack.enter_context(nc.sbuf_tensor("sb_" + name, list(shape), dt))

    ident = sb("ident", [128, 128], BF16); t_ident = k.tile("ident", dma=True)
    norms = sb("norms", [128, 7, KC], F32); t_norms = k.tile("norms", dma=True)
    xt = sb("xt", [128, NS, D], F32)
    t_xt = [k.tile(f"xt{s}", dma=True) for s in range(NS)]
    hT = sb("hT", [128, KC, TT], BF16)
    t_hT = [k.tile(f"hT{s}") for s in range(NS)]
    NRING = 4
    wring = [sb(f"wring{i}", [128, KC, 256], BF16) for i in range(NRING)]
    t_wring = [k.tile(f"wring{i}", dma=True) for i in range(NRING)]
    ring_pos = [0]
    hb = [sb(f"hb{i}", [128, D], BF16) for i in range(2)]
    t_hb = [k.tile(f"hb{i}") for i in range(2)]
    junk = sb("junk", [128, D], BF16); t_junk = k.tile("junk")
    sg = [sb(f"sg{i}", [128, 512], F32) for i in range(2)]
    t_sg = [k.tile(f"sg{i}") for i in range(2)]
    ssq = sb("ssq", [128, 3 * NS], F32); t_ssq = [k.tile(f"ssq{s}") for s in range(NS)]
    rstd = sb("rstd", [128, NS], F32); t_rstd = [k.tile(f"rstd{s}") for s in range(NS)]
    epsc = sb("epsc", [128, 2], F32); t_epsc = k.tile("epsc")
    tri = sb("tri", [128, 128], F32); t_tri = k.tile("tri", dma=True)
    maskT = sb("maskT", [128, 128], F32); t_maskT = k.tile("maskT", dma=True)
    negmask = sb("negmask", [128, 128], BF16); t_negmask = k.tile("negmask", dma=True)
    maskTb = sb("maskTb", [128, 128], BF16); t_maskTb = k.tile("maskTb")
    zer = sb("zer", [128, 512], BF16); t_zer = k.tile("zer")
    wg2b = sb("wg2b", [17, GLA_DK], BF16); t_wg2b = k.tile("wg2b", dma=True)
    trib = sb("trib", [128, 128], BF16); t_trib = k.tile("trib")
    ggout = sb("ggout", [128, 256], F32); t_ggout = k.tile("ggout", dma=True)
    dgout = sb("dgout", [128, 128], F32); t_dgout = k.tile("dgout", dma=True)
    lamv = sb("lamv", [128, 4, 64], F32); t_lamv = k.tile("lamv", dma=True)
    lams = sb("lams", [128, 8], F32); t_lams = k.tile("lams")
    Sst = sb("Sst", [128, 4, 256], F32); t_S = [k.tile(f"S{h}") for h in range(4)]
    Sbf = sb("Sbf", [128, 4, 256], BF16); t_Sbf = [k.tile(f"Sbf{h}") for h in range(4)]
    small = sb("small", [128, 64], F32); t_small = k.tile("small")
    rem = nc.sbuf_bytes_remaining
    print(f"[build] sbuf remaining before work: {rem}, work={WORK_BYTES}")
    work = sb("work", [128, WORK_BYTES // 2], BF16)

    def carve(lo, nbytes, dt, pattern=None, **kw):
        ap = work[:, lo // 2:(lo + nbytes) // 2]
        if dt == F32:
            ap = ap.bitcast(F32)
        if pattern:
            ap = ap.rearrange(pattern, **kw)
        return ap

    psall = stack.enter_context(nc.psum_tensor("ps_all", [128, 8, 512], F32))
    bank = [psall[:, i, :] for i in range(8)]
    t_bank = [k.tile(f"bank{i}") for i in range(8)]

    def bank_bf(i, pattern=None, **kw):
        ap = bank[i].bitcast(BF16)
        if pattern:
            ap = ap.rearrange(pattern, **kw)
        return ap

    cnt = {"gu": 0, "o": 0, "t": 0, "hb": 0, "sg": 0, "fm": 0, "tm": 0}

    k.op(DVE, lambda: nc.vector.memset(epsc[:, 0:1], EPS), writes=[t_epsc])
    k.op(DVE, lambda: nc.vector.memset(epsc[:, 1:2], 1.0), writes=[t_epsc])
    k.dma(SP, t_ident, None, [(ident[:], ident_d)])
    k.dma(SP, t_norms, None, [(norms[:], norms_d)])
    k.dma(SP, t_tri, None, [(tri[:], tri_d)])
    k.dma(SP, t_maskT, None, [(maskT[:], maskT_d)])
    k.dma(SP, t_negmask, None, [(negmask[:], negmask_d)])
    k.op(DVE, lambda: nc.vector.tensor_copy(out=maskTb[:], in_=maskT[:]), reads=[t_maskT], writes=[t_maskTb])
    k.op(DVE, lambda: nc.vector.memset(zer[:], 0.0), writes=[t_zer])
    k.dma(POOL, t_wg2b, None, [(wg2b[:], gla_wg2b_d)])
    k.op(DVE, lambda: nc.vector.tensor_copy(out=trib[:], in_=tri[:]), reads=[t_tri], writes=[t_trib])
    k.dma(SP, t_ggout, None, [(ggout[:], gla_gout_d)])
    k.dma(SP, t_dgout, None, [(dgout[:], diff_gout_d)])
    k.dma(SP, t_lamv, None, [(lamv[:], diff_lam_d)])
    for h in range(4):
        k.op(DVE, lambda: nc.vector.memset(Sst[:, h, :], 0.0), writes=[t_S[h]])
        k.op(DVE, lambda: nc.vector.memset(Sbf[:, h, :], 0.0), writes=[t_Sbf[h]])
    k.op(DVE, lambda: nc.vector.tensor_scalar(out=dgout[:], in0=dgout[:], scalar1=1.0 - LAMBDA_INIT, scalar2=None,
                                              op0=ALU.mult), reads=[t_dgout], writes=[t_dgout])
    k.op(DVE, lambda: nc.vector.tensor_tensor(out=lamv[:, 0, :], in0=lamv[:, 0, :], in1=lamv[:, 1, :], op=ALU.mult),
         reads=[t_lamv], writes=[t_lamv])
    k.op(DVE, lambda: nc.vector.tensor_tensor(out=lamv[:, 2, :], in0=lamv[:, 2, :], in1=lamv[:, 3, :], op=ALU.mult),
         reads=[t_lamv], writes=[t_lamv])
    k.op(DVE, lambda: nc.vector.reduce_sum(out=lams[:, 0:1], in_=lamv[:, 0, :], axis=mybir.AxisListType.X),
         reads=[t_lamv], writes=[t_lams])
    k.op(DVE, lambda: nc.vector.reduce_sum(out=lams[:, 1:2], in_=lamv[:, 2, :], axis=mybir.AxisListType.X),
         reads=[t_lamv], writes=[t_lams])
    k.op(ACT, lambda: nc.scalar.activation(out=lams[:, 2:4], in_=lams[:, 0:2], func=AF.Exp),
         reads=[t_lams], writes=[t_lams])
    k.op(DVE, lambda: nc.vector.scalar_tensor_tensor(out=lams[:, 4:5], in0=lams[:, 3:4], scalar=-LAMBDA_INIT,
                                                     in1=lams[:, 2:3], op0=ALU.add, op1=ALU.subtract),
         reads=[t_lams], writes=[t_lams])

    def load_x(tile_i):
        for s in range(NS):
            r0 = tile_i * TT + s * 128
            k.dma(SP, t_xt[s], None, [(xt[:, s, :], x_d[r0:r0 + 128, :])])

    def norm_sub_stats(s):
        k.op(ACT, lambda: nc.scalar.activation(out=junk[:], in_=xt[:, s, :], func=AF.Square,
                                               accum_out=ssq[:, s:s + 1]),
             reads=[t_xt[s]], writes=[t_junk, t_ssq[s]])
        k.op(ACT, lambda: nc.scalar.activation(out=ssq[:, NS + s:NS + s + 1], in_=ssq[:, s:s + 1], func=AF.Ln,
                                               bias=epsc[:, 0:1], scale=1.0 / D),
             reads=[t_ssq[s], t_epsc], writes=[t_ssq[s]])
        k.op(ACT, lambda: nc.scalar.activation(out=rstd[:, s:s + 1], in_=ssq[:, NS + s:NS + s + 1], func=AF.Exp,
                                               scale=-0.5),
             reads=[t_ssq[s]], writes=[t_rstd[s]])

    def rms_stats():
        for s in range(NS):
            norm_sub_stats(s)

    def norm_sub_hb(s, on_act=False):
        i_hb = cnt["hb"] % 2; cnt["hb"] += 1
        if on_act:
            k.op(ACT, lambda: nc.scalar.activation(out=hb[i_hb][:], in_=xt[:, s, :], func=AF.Copy,
                                                   scale=rstd[:, s:s + 1]),
                 reads=[t_rstd[s], t_xt[s]], writes=[t_hb[i_hb]])
        else:
            k.op(DVE, lambda: nc.vector.tensor_scalar(out=hb[i_hb][:], in0=xt[:, s, :], scalar1=rstd[:, s:s + 1],
                                                      scalar2=None, op0=ALU.mult),
                 reads=[t_rstd[s], t_xt[s]], writes=[t_hb[i_hb]])
        return i_hb

    def norm_sub_tr(s, i_hb, n_idx):
        transpose_to_hT(lambda kc: hb[i_hb][:, kc * 128:(kc + 1) * 128], t_hb[i_hb], s, gscale_idx=n_idx)

    def transpose_to_hT(src_ap_fn, src_tile, s, gscale_idx=None, use_bank=None):
        if use_bank is None:
            i_t = 6 + cnt["t"] % 2; cnt["t"] += 1
        else:
            i_t = use_bank
        pt = bank_bf(i_t, "p (a b) -> p a b", a=KC)
        for kc in range(KC):
            k.op(PE, lambda: nc.tensor.transpose(pt[:, kc, :], src_ap_fn(kc), ident[:]),
                 reads=[src_tile, t_ident], writes=[t_bank[i_t]], inc=(kc == KC - 1))
        if gscale_idx is None:
            k.op(ACT, lambda: nc.scalar.copy(out=hT[:, :, s * 128:(s + 1) * 128], in_=pt[:, :, :]),
                 reads=[t_bank[i_t]], writes=[t_hT[s], t_bank[i_t]])
        else:
            k.op(DVE, lambda: nc.vector.tensor_tensor(out=hT[:, :, s * 128:(s + 1) * 128], in0=pt[:, :, :],
                                                      in1=norms[:, gscale_idx, :].unsqueeze(2).to_broadcast(
                                                          [128, KC, 128]),
                                                      op=ALU.mult),
                 reads=[t_bank[i_t], t_norms], writes=[t_hT[s], t_bank[i_t]])

    def norm_to_hT(n_idx):
        rms_stats()
        for s in range(NS):
            i_hb = norm_sub_hb(s)
            norm_sub_tr(s, i_hb, n_idx)

    pre_done = {"v": False}

    def phase_norm(n_idx):
        if pre_done["v"]:
            pre_done["v"] = False
        else:
            norm_to_hT(n_idx)

    def ring_load(w_d, col_specs):
        i = ring_pos[0] % NRING; ring_pos[0] += 1
        pairs = []
        for (d0, s0, n) in col_specs:
            pairs.append((wring[i][:, :, d0:d0 + n],
                          w_d[:, s0:s0 + n].rearrange("(kc p) c -> p kc c", p=128)))
        k.dma(POOL, t_wring[i], None, pairs)
        return i

    def proj_featmajor(w_d, col0, nchunks, consumer, halves=(0, 1), banks=(0, 1, 2, 3)):
        for c0 in range(0, nchunks, 2):
            ncol = min(2, nchunks - c0) * 128
            i = ring_load(w_d, [(0, col0 + c0 * 128, ncol)])
            for cc in range(ncol // 128):
                for hf in halves:
                    b = banks[cnt["fm"] % len(banks)]; cnt["fm"] += 1
                    for kc in range(KC):
                        k.op(PE, lambda: nc.tensor.matmul(bank[b][:], lhsT=wring[i][:, kc, cc * 128:(cc + 1) * 128],
                                                          rhs=hT[:, kc, hf * 512:(hf + 1) * 512],
                                                          start=(kc == 0), stop=(kc == KC - 1)),
                             reads=[t_wring[i]] + t_hT[hf * 4:(hf + 1) * 4], writes=[t_bank[b]],
                             inc=(kc == KC - 1))
                    consumer(c0 + cc, hf, b)

    def proj_tokmajor(w_d, col0, npanels, consumer, banks=(4, 5)):
        for p in range(npanels):
            i = ring_load(w_d, [(0, col0 + p * 256, 256)])
            for s in range(NS):
                b = banks[cnt["tm"] % len(banks)]; cnt["tm"] += 1
                for kc in range(KC):
                    k.op(PE, lambda: nc.tensor.matmul(bank[b][:, 0:256], lhsT=hT[:, kc, s * 128:(s + 1) * 128],
                                                      rhs=wring[i][:, kc, :],
                                                      start=(kc == 0), stop=(kc == KC - 1)),
                         reads=[t_wring[i], t_hT[s]], writes=[t_bank[b]], inc=(kc == KC - 1))
                consumer(p, s, b)

    def resid_add_consumer(p, s, b):
        k.op(DVE, lambda: nc.vector.tensor_tensor(out=xt[:, s, p * 256:(p + 1) * 256], in0=bank[b][:, 0:256],
                                                  in1=xt[:, s, p * 256:(p + 1) * 256], op=ALU.add),
             reads=[t_bank[b], t_xt[s]], writes=[t_xt[s], t_bank[b]])

    wdn_sems = [k.dsem(f"wdn{c}") for c in range(NHC)]
    out_sems = [k.dsem(f"out{i}") for i in range(2)]
    gf_sem = k.dsem("gfin")
    last_out = []

    def ffn(fi, n_idx, nxt=None, ti=None, has_next=False):
        w_in, w_dn = ffn_w_in[fi], ffn_w_down[fi]
        act = carve(0, A_BYTES, BF16, "p (c t) -> p c t", c=NHC)
        t_act = [[k.view(f"act{c}_{h}", (c * TT + h * 512) * 2, (c * TT + h * 512 + 512) * 2) for h in range(2)]
                 for c in range(NHC)]
        wdn = carve(A_BYTES, W_BYTES, BF16, "p (c t) -> p c t", c=NHC)
        t_wdn = [k.view(f"wdn{c}", A_BYTES + c * D * 2, A_BYTES + (c + 1) * D * 2, dsem=wdn_sems[c])
                 for c in range(NHC)]
        phase_norm(n_idx)
        PRE = 3
        slots = {}
        for c in range(min(PRE, NHC)):
            slots[c] = ring_load(w_in, [(0, c * 128, 128), (128, DFF + c * 128, 128)])
        for c in range(NHC):
            if c + PRE < NHC:
                slots[c + PRE] = ring_load(w_in, [(0, (c + PRE) * 128, 128), (128, DFF + (c + PRE) * 128, 128)])
            k.dma(POOL, t_wdn[c], None, [(wdn[:, c, :], w_dn[c * 128:(c + 1) * 128, :])])
            i = slots[c]
            for hf in range(2):
                j = cnt["gu"] % 2; cnt["gu"] += 1
                bg, bu = j, 2 + j
                hts = t_hT[hf * 4:(hf + 1) * 4]
                for kc in range(KC):
                    k.op(PE, lambda: nc.tensor.matmul(bank[bg][:], lhsT=wring[i][:, kc, 0:128],
                                                      rhs=hT[:, kc, hf * 512:(hf + 1) * 512],
                                                      start=(kc == 0), stop=(kc == KC - 1)),
                         reads=[t_wring[i]] + hts, writes=[t_bank[bg]], inc=(kc == KC - 1))
                for kc in range(KC):
                    k.op(PE, lambda: nc.tensor.matmul(bank[bu][:], lhsT=wring[i][:, kc, 128:256],
                                                      rhs=hT[:, kc, hf * 512:(hf + 1) * 512],
                                                      start=(kc == 0), stop=(kc == KC - 1)),
                         reads=[t_wring[i]] + hts, writes=[t_bank[bu]], inc=(kc == KC - 1))
                q = cnt["sg"] % 2; cnt["sg"] += 1
                k.op(ACT, lambda: nc.scalar.activation(out=sg[q][:], in_=bank[bg][:], func=AF.Silu),
                     reads=[t_bank[bg]], writes=[t_sg[q], t_bank[bg]])
                k.op(DVE, lambda: nc.vector.tensor_tensor(out=act[:, c, hf * 512:(hf + 1) * 512],
                                                          in0=bank[bu][:], in1=sg[q][:], op=ALU.mult),
                     reads=[t_sg[q], t_bank[bu]], writes=[t_act[c][hf], t_bank[bu]])
        pend = None
        fin = {}
        if nxt == "final":
            B0f = A_BYTES + W_BYTES
            gf = carve(B0f, 4096, F32); t_gf = k.view("gfin", B0f, B0f + 4096, dsem=gf_sem)
            obf = [carve(B0f + 4096 + i * 4096, 4096, F32) for i in range(2)]
            k.dma(SP, t_gf, None, [(gf[:], gfin_d)])

        def fin_step(step):
            a, b, c = step - 1, step - 2, step - 3
            if 0 <= a < NS:
                norm_sub_stats(a)
                i = a % 2
                t_ob = k.view(f"ob{i}", B0f + 4096 + i * 4096, B0f + 8192 + i * 4096, dsem=None)
                k.op(DVE, lambda: nc.vector.scalar_tensor_tensor(out=obf[i][:], in0=xt[:, a, :],
                                                                 scalar=rstd[:, a:a + 1], in1=gf[:],
                                                                 op0=ALU.mult, op1=ALU.mult),
                     reads=[t_xt[a], t_rstd[a], t_gf], writes=[t_ob])
                r0 = ti * TT + a * 128
                t_out = Tl(f"o{ti}_{a}", out_sems[i])
                k.dma(SP, t_out, t_ob, [(out_d[r0:r0 + 128, :], obf[i][:])])
                last_out.append(t_out)
                if has_next:
                    r1 = (ti + 1) * TT + a * 128
                    k.dma(SP, t_xt[a], None, [(xt[:, a, :], x_d[r1:r1 + 128, :])])
            if has_next and 0 <= b < NS:
                norm_sub_stats(b)
                fin[b] = norm_sub_hb(b)
            if has_next and 0 <= c < NS:
                norm_sub_tr(c, fin[c], 0)

        for s in range(NS):
            if nxt == "final":
                fin_step(s)
            elif s > 0 and nxt is not None:
                norm_sub_stats(s - 1)
                if pend is not None:
                    norm_sub_tr(*pend)
                pend = (s - 1, norm_sub_hb(s - 1), nxt)
            for of in range(2):
                j = 4 + cnt["o"] % 2; cnt["o"] += 1
                for c in range(NHC):
                    k.op(PE, lambda: nc.tensor.matmul(bank[j][:], lhsT=act[:, c, s * 128:(s + 1) * 128],
                                                      rhs=wdn[:, c, of * 512:(of + 1) * 512],
                                                      start=(c == 0), stop=(c == NHC - 1)),
                         reads=[t_act[c][s // 4], t_wdn[c]], writes=[t_bank[j]], inc=(c == NHC - 1))
                k.op(DVE, lambda: nc.vector.scalar_tensor_tensor(out=xt[:, s, of * 512:(of + 1) * 512],
                                                                 in0=bank[j][:], scalar=0.5,
                                                                 in1=xt[:, s, of * 512:(of + 1) * 512],
                                                                 op0=ALU.mult, op1=ALU.add),
                     reads=[t_bank[j], t_xt[s]], writes=[t_xt[s], t_bank[j]])

        if nxt == "final":
            for st in range(NS, NS + 3):
                fin_step(st)
            pre_done["v"] = has_next
        elif nxt is not None:
            norm_sub_stats(NS - 1)
            if pend is not None:
                norm_sub_tr(*pend)
            norm_sub_tr(NS - 1, norm_sub_hb(NS - 1), nxt)
            pre_done["v"] = True

    wo_sems = [k.dsem(f"wo{p}") for p in range(4)]

    def wout_load(w_d):
        wo = [carve(A_BYTES + p * 4096, 4096, BF16, "p (kc c) -> p kc c", kc=KC) for p in range(4)]
        t_wo = [k.view(f"wo{p}", A_BYTES + p * 4096, A_BYTES + (p + 1) * 4096, dsem=wo_sems[p]) for p in range(4)]
        for p in range(4):
            k.dma(POOL, t_wo[p], None,
                  [(wo[p][:, :, :], w_d[:, p * 256:(p + 1) * 256].rearrange("(kc p) c -> p kc c", p=128))])
        return wo, t_wo

    def out_proj_fused(wo, t_wo, nxt):
        pend = None
        for s in range(NS):
            for p in range(4):
                b = 4 + cnt["tm"] % 2; cnt["tm"] += 1
                for kc in range(KC):
                    k.op(PE, lambda: nc.tensor.matmul(bank[b][:, 0:256], lhsT=hT[:, kc, s * 128:(s + 1) * 128],
                                                      rhs=wo[p][:, kc, :], start=(kc == 0), stop=(kc == KC - 1)),
                         reads=[t_wo[p], t_hT[s]], writes=[t_bank[b]], inc=(kc == KC - 1))
                resid_add_consumer(p, s, b)
            norm_sub_stats(s)
            if pend is not None:
                norm_sub_tr(*pend)
            pend = (s, norm_sub_hb(s, on_act=True), nxt)
        norm_sub_tr(*pend)
        pre_done["v"] = True

    def gla(n_idx, nxt):
        B0 = A_BYTES + W_BYTES
        Eb = carve(0, 16384, F32, "p (h t) -> p h t", h=4)
        t_Eb = [k.view(f"Eb{s}", s * 2048, (s + 1) * 2048) for s in range(NS)]
        Ei = carve(16384, 16384, F32, "p (h t) -> p h t", h=4)
        t_Ei = [k.view(f"Ei{s}", 16384 + s * 2048, 16384 + (s + 1) * 2048) for s in range(NS)]
        qd = carve(32768, 8192, BF16, "p (h t) -> p h t", h=4)
        t_qd = [[k.view(f"qd{h}_{f}", 32768 + (h * TT + f * 512) * 2, 32768 + (h * TT + f * 512 + 512) * 2)
                 for f in range(2)] for h in range(4)]
        glr = carve(40960, 2048, BF16); t_glr = k.view("glr", 40960, 43008)
        sph = [carve(43008 + i * 1024, 1024, BF16) for i in range(2)]
        t_sph = [k.view(f"sph{i}", 43008 + i * 1024, 43008 + (i + 1) * 1024) for i in range(2)]
        ki = carve(B0, 8192, BF16, "p (h t) -> p h t", h=4)
        t_ki = [[k.view(f"ki{h}_{f}", B0 + (h * TT + f * 512) * 2, B0 + (h * TT + f * 512 + 512) * 2)
                 for f in range(2)] for h in range(4)]
        kt = carve(B0 + 8192, 8192, BF16, "p (s d) -> p s d", s=NS)
        t_kt = [k.view(f"kt{s}", B0 + 8192 + s * 1024, B0 + 8192 + (s + 1) * 1024) for s in range(NS)]
        spb = [carve(B0 + 16384 + i * 2048, 2048, F32) for i in range(2)]
        t_sp = [k.view(f"sp{i}", B0 + 16384 + i * 2048, B0 + 16384 + (i + 1) * 2048) for i in range(2)]
        og = [carve(B0 + 20480 + i * 2048, 2048, BF16) for i in range(2)]
        t_og = [k.view(f"og{i}", B0 + 20480 + i * 2048, B0 + 20480 + (i + 1) * 2048) for i in range(2)]
        attm = [carve(B0 + 24576 + i * 1024, 1024, BF16, "p (h t) -> p h t", h=4) for i in range(2)]
        t_attm = [k.view(f"attm{i}", B0 + 24576 + i * 1024, B0 + 24576 + (i + 1) * 1024) for i in range(2)]
        dec = carve(B0 + 26624, 128, F32, "p (s h) -> p s h", s=NS)
        t_dec = [k.view(f"dec{s}", B0 + 26624 + s * 16, B0 + 26624 + (s + 1) * 16) for s in range(NS)]
        spl = [carve(B0 + 29184 + i * 1024, 1024, BF16) for i in range(2)]
        t_spl = [k.view(f"spl{i}", B0 + 29184 + i * 1024, B0 + 29184 + (i + 1) * 1024) for i in range(2)]
        otmp = [carve(B0 + 27136 + i * 1024, 1024, F32) for i in range(2)]
        t_otmp = [k.view(f"otmp{i}", B0 + 27136 + i * 1024, B0 + 27136 + (i + 1) * 1024) for i in range(2)]

        phase_norm(n_idx)
        k.op(DVE, lambda: nc.vector.memset(glr[0:17, :], 1.0), writes=[t_glr])

        i = ring_load(gla_w_in_d, [(0, 2 * GLA_DK + 2 * GLA_DV, GLA_R)])
        for hf in range(2):
            b = hf
            for kc in range(KC):
                k.op(PE, lambda: nc.tensor.matmul(bank[b][0:16, :], lhsT=wring[i][:, kc, 0:16],
                                                  rhs=hT[:, kc, hf * 512:(hf + 1) * 512],
                                                  start=(kc == 0), stop=(kc == KC - 1)),
                     reads=[t_wring[i]] + t_hT[hf * 4:(hf + 1) * 4], writes=[t_bank[b]], inc=(kc == KC - 1))
            k.op(DVE, lambda: nc.vector.tensor_copy(out=glr[0:16, hf * 512:(hf + 1) * 512], in_=bank[b][0:16, :]),
                 reads=[t_bank[b]], writes=[t_glr, t_bank[b]])
        def gate_A(s):
            bz = 2 + s % 2
            q = s % 2
            k.op(PE, lambda: nc.tensor.matmul(bank[bz][:], lhsT=glr[0:17, s * 128:(s + 1) * 128], rhs=wg2b[0:17, :],
                                              start=True, stop=True),
                 reads=[t_glr, t_wg2b], writes=[t_bank[bz]])
            k.op(ACT, lambda: nc.scalar.activation(out=spb[q][:], in_=bank[bz][:], func=AF.Exp, scale=-1.0),
                 reads=[t_bank[bz]], writes=[t_sp[q], t_bank[bz]])
            k.op(ACT, lambda: nc.scalar.activation(out=spb[q][:], in_=spb[q][:], func=AF.Ln, bias=epsc[:, 1:2]),
                 reads=[t_sp[q], t_epsc], writes=[t_sp[q]])

        def gate_B(s):
            q = s % 2
            bb = 4 + s % 2
            bT = bank[bb][:].rearrange("p (h t) -> p h t", h=4)
            k.op(DVE, lambda: nc.vector.tensor_copy(out=sph[q][:], in_=spb[q][:]),
                 reads=[t_sp[q]], writes=[t_sph[q]])
            k.op(DVE, lambda: nc.vector.tensor_tensor(out=spl[q][:], in0=spb[q][:], in1=sph[q][:],
                                                      op=ALU.subtract),
                 reads=[t_sp[q], t_sph[q]], writes=[t_spl[q]])
            for h in range(4):
                k.op(PE, lambda: nc.tensor.matmul(bT[:, h, :], lhsT=sph[q][:, h * 128:(h + 1) * 128], rhs=trib[:],
                                                  start=True, stop=False),
                     reads=[t_sph[q], t_trib], writes=[t_bank[bb]], inc=False)
                k.op(PE, lambda: nc.tensor.matmul(bT[:, h, :], lhsT=spl[q][:, h * 128:(h + 1) * 128], rhs=trib[:],
                                                  start=False, stop=True),
                     reads=[t_spl[q], t_trib], writes=[t_bank[bb]], inc=(h == 3))
            k.op(ACT, lambda: nc.scalar.activation(out=Eb[:, :, s * 128:(s + 1) * 128], in_=bT, func=AF.Exp),
                 reads=[t_bank[bb]], writes=[t_Eb[s], t_bank[bb]])
            k.op(ACT, lambda: nc.scalar.activation(out=Ei[:, :, s * 128:(s + 1) * 128], in_=bT, func=AF.Exp,
                                                   scale=-1.0),
                 reads=[t_bank[bb]], writes=[t_Ei[s], t_bank[bb]])
            k.op(DVE, lambda: nc.vector.tensor_copy(out=dec[:, s, :], in_=Eb[:, :, s * 128 + 127]),
                 reads=[t_Eb[s]], writes=[t_dec[s]])

        gate_A(0)
        for s in range(NS):
            if s + 1 < NS:
                gate_A(s + 1)
            gate_B(s)

        def q_cons(c, hf, b):
            k.op(DVE, lambda: nc.vector.scalar_tensor_tensor(out=qd[:, c, hf * 512:(hf + 1) * 512], in0=bank[b][:],
                                                             scalar=128.0 ** -0.5,
                                                             in1=Eb[:, c, hf * 512:(hf + 1) * 512],
                                                             op0=ALU.mult, op1=ALU.mult),
                 reads=[t_bank[b]] + t_Eb[hf * 4:(hf + 1) * 4], writes=[t_qd[c][hf], t_bank[b]])

        def k_cons(c, hf, b):
            k.op(DVE, lambda: nc.vector.tensor_tensor(out=ki[:, c, hf * 512:(hf + 1) * 512], in0=bank[b][:],
                                                      in1=Ei[:, c, hf * 512:(hf + 1) * 512], op=ALU.mult),
                 reads=[t_bank[b]] + t_Ei[hf * 4:(hf + 1) * 4], writes=[t_ki[c][hf], t_bank[b]])

        proj_featmajor(gla_w_in_d, 0, 4, q_cons)
        proj_featmajor(gla_w_in_d, GLA_DK, 4, k_cons)
        for s in range(NS):
            i_t = 6 + cnt["t"] % 2; cnt["t"] += 1
            pt = bank_bf(i_t, "p (a b) -> p a b", a=KC)
            for h in range(4):
                k.op(PE, lambda: nc.tensor.transpose(pt[:, h, :], ki[:, h, s * 128:(s + 1) * 128], ident[:]),
                     reads=[t_ki[h][s // 4], t_ident], writes=[t_bank[i_t]], inc=(h == 3))
            k.op(ACT, lambda: nc.scalar.copy(out=kt[:, s, :].rearrange("p (h d) -> p h d", h=4), in_=pt[:, 0:4, :]),
                 reads=[t_bank[i_t]], writes=[t_kt[s], t_bank[i_t]])
        vv = carve(0, 16384, BF16, "p (s c) -> p s c", s=NS)
        t_vv = [k.view(f"vv{s}", s * 2048, (s + 1) * 2048) for s in range(NS)]
        sr = carve(16384, 16384, BF16, "p (s c) -> p s c", s=NS)
        t_sr = [k.view(f"sr{s}", 16384 + s * 2048, 16384 + (s + 1) * 2048) for s in range(NS)]

        def v_cons(p, s, b):
            k.op(ACT, lambda: nc.scalar.copy(out=vv[:, s, p * 256:(p + 1) * 256], in_=bank[b][:, 0:256]),
                 reads=[t_bank[b]], writes=[t_vv[s], t_bank[b]])

        def r_cons(p, s, b):
            q = cnt["sg"] % 2; cnt["sg"] += 1
            k.op(ACT, lambda: nc.scalar.activation(out=sg[q][:, 0:256], in_=bank[b][:, 0:256], func=AF.Silu),
                 reads=[t_bank[b]], writes=[t_sg[q], t_bank[b]])
            k.op(DVE, lambda: nc.vector.tensor_tensor(out=sr[:, s, p * 256:(p + 1) * 256], in0=sg[q][:, 0:256],
                                                      in1=ggout[:], op=ALU.mult),
                 reads=[t_sg[q], t_ggout], writes=[t_sr[s]])

        proj_tokmajor(gla_w_in_d, 2 * GLA_DK, 4, v_cons)
        proj_tokmajor(gla_w_in_d, 2 * GLA_DK + GLA_DV, 4, r_cons)
        wo, t_wo = wout_load(gla_w_out_d)

        at4 = bank[0].rearrange("p (h t) -> p h t", h=4)
        pb = [bank[4].rearrange("p (h v) -> p h v", h=2), bank[5].rearrange("p (h v) -> p h v", h=2)]

        def obank(s, h):
            bi = (2 if s % 2 == 0 else 6) + h // 2
            return bank[bi].rearrange("p (h v) -> p h v", h=2)[:, h % 2, :], t_bank[bi]

        def S1(s):
            f = s // 4
            sl = slice(s * 128, (s + 1) * 128)
            am = s % 2
            for h in range(4):
                k.op(PE, lambda: nc.tensor.matmul(at4[:, h, :], lhsT=ki[:, h, sl], rhs=qd[:, h, sl],
                                                  start=True, stop=True),
                     reads=[t_ki[h][f], t_qd[h][f]], writes=[t_bank[0]], inc=(h == 3))
            k.op(DVE, lambda: nc.vector.tensor_tensor(out=attm[am][:], in0=at4,
                                                      in1=maskT[:].unsqueeze(1).to_broadcast([128, 4, 128]),
                                                      op=ALU.mult),
                 reads=[t_bank[0], t_maskT], writes=[t_attm[am], t_bank[0]])

        def S2(s):
            f = s // 4
            sl = slice(s * 128, (s + 1) * 128)
            am = s % 2
            for h in range(4):
                k.op(PE, lambda: nc.tensor.matmul(pb[h // 2][:, h % 2, :], lhsT=kt[:, s, h * 128:(h + 1) * 128],
                                                  rhs=vv[:, s, h * 256:(h + 1) * 256],
                                                  start=True, stop=True),
                     reads=[t_kt[s], t_vv[s]], writes=[t_bank[4 + h // 2]], inc=(h % 2 == 1))
            for h in range(4):
                o_ap, t_o = obank(s, h)
                k.op(PE, lambda: nc.tensor.matmul(o_ap, lhsT=attm[am][:, h, :],
                                                  rhs=vv[:, s, h * 256:(h + 1) * 256],
                                                  start=True, stop=False),
                     reads=[t_attm[am], t_vv[s]], writes=[t_o], inc=False)
                k.op(PE, lambda: nc.tensor.matmul(o_ap, lhsT=qd[:, h, sl], rhs=Sbf[:, h, :],
                                                  start=False, stop=True),
                     reads=[t_qd[h][f], t_Sbf[h]], writes=[t_o], inc=(h % 2 == 1))
            for h in range(4):
                k.op(DVE, lambda: nc.vector.tensor_tensor(out=Sst[:, h, :], in0=pb[h // 2][:, h % 2, :],
                                                          in1=Sst[:, h, :], op=ALU.add),
                     reads=[t_bank[4 + h // 2], t_S[h]], writes=[t_S[h], t_bank[4 + h // 2]])
                k.op(DVE, lambda: nc.vector.tensor_scalar(out=Sst[:, h, :], in0=Sst[:, h, :],
                                                          scalar1=dec[:, s, h:h + 1], scalar2=None, op0=ALU.mult),
                     reads=[t_S[h], t_dec[s]], writes=[t_S[h]])
                k.op(POOL, lambda: nc.gpsimd.tensor_copy(out=Sbf[:, h, :], in_=Sst[:, h, :]),
                     reads=[t_S[h]], writes=[t_Sbf[h]])

        def S3(s):
            io = s % 2
            for h in range(4):
                o_ap, t_o = obank(s, h)
                k.op(ACT, lambda: nc.scalar.activation(out=junk[:, 0:256], in_=o_ap,
                                                       func=AF.Square, accum_out=small[:, h:h + 1]),
                     reads=[t_o], writes=[t_junk, t_small, t_o])
            k.op(ACT, lambda: nc.scalar.activation(out=small[:, 4:8], in_=small[:, 0:4], func=AF.Ln,
                                                   bias=epsc[:, 0:1], scale=1.0 / 256),
                 reads=[t_small, t_epsc], writes=[t_small])
            k.op(ACT, lambda: nc.scalar.activation(out=small[:, 8:12], in_=small[:, 4:8], func=AF.Exp, scale=-0.5),
                 reads=[t_small], writes=[t_small])
            for h in range(4):
                o_ap, t_o = obank(s, h)
                k.op(DVE, lambda: nc.vector.scalar_tensor_tensor(out=og[io][:, h * 256:(h + 1) * 256], in0=o_ap,
                                                                 scalar=small[:, 8 + h:9 + h],
                                                                 in1=sr[:, s, h * 256:(h + 1) * 256],
                                                                 op0=ALU.mult, op1=ALU.mult),
                     reads=[t_o, t_small, t_sr[s]], writes=[t_og[io], t_o])

        def S4(s):
            io = s % 2
            transpose_to_hT(lambda kc: og[io][:, kc * 128:(kc + 1) * 128], t_og[io], s, use_bank=1)

        S1(0)
        for s in range(NS):
            if s + 1 < NS:
                S1(s + 1)
            S2(s)
            if s >= 1:
                S3(s - 1)
            if s >= 2:
                S4(s - 2)
        S3(NS - 1)
        S4(NS - 2)
        S4(NS - 1)
        out_proj_fused(wo, t_wo, nxt)

    kst_sem = [k.dsem(f"kst{i}") for i in range(2)]
    vst_sem = [k.dsem(f"vst{i}") for i in range(4)]
    kc_list = [[] for _ in range(8)]
    vc_list = [[] for _ in range(8)]
    ld_sems = [k.dsem(f"kld{i}") for i in range(2)] + [k.dsem(f"vld{i}") for i in range(2)]

    def diff(n_idx, ti, nxt):
        B0 = A_BYTES + W_BYTES
        qT = carve(0, 16384, BF16, "p (h t) -> p h t", h=8)
        t_qT = [[k.view(f"qT{h}_{f}", (h * TT + f * 512) * 2, (h * TT + f * 512 + 512) * 2) for f in range(2)]
                for h in range(8)]
        kTh = [carve(16384 + i * 8192, 8192, BF16) for i in range(2)]
        t_kTh = [k.view(f"kTh{i}", 16384 + i * 8192, 16384 + (i + 1) * 8192, dsem=ld_sems[i]) for i in range(2)]
        kst = [carve(32768 + i * 2048, 2048, BF16) for i in range(2)]
        t_kst = [k.view(f"kst{i}", 32768 + i * 2048, 32768 + (i + 1) * 2048) for i in range(2)]
        vst = [carve(36864 + i * 520, 520, BF16, "p (h c) -> p h c", h=2) for i in range(4)]
        t_vst = [k.view(f"vst{i}", 36864 + i * 520, 36864 + (i + 1) * 520) for i in range(4)]
        osb = carve(39936, 4160, F32, "p (j m c) -> p j m c", j=4, m=2)
        t_osb = [k.view(f"osb{j}", 39936 + j * 1040, 39936 + (j + 1) * 1040) for j in range(4)]
        pT = [carve(B0 + i * 2048, 2048, BF16, "p (m q) -> p m q", m=2) for i in range(3)]
        t_pT = [[k.view(f"pT{i}_{j}", B0 + i * 2048 + j * 512, B0 + i * 2048 + (j + 1) * 512) for j in range(4)]
                for i in range(3)]
        VB = 8320
        Vh = [carve(B0 + 6144 + i * VB, VB, BF16, "p (b c) -> p b c", c=130) for i in range(2)]
        t_Vh = [k.view(f"Vh{i}", B0 + 6144 + i * VB, B0 + 6144 + (i + 1) * VB, dsem=ld_sems[2 + i])
                for i in range(2)]
        ogo = [B0 + 6144 + 2 * VB, A_BYTES + W_BYTES - 8192]
        og = [carve(o_, 8192, BF16, "p (j c) -> p j c", j=4) for o_ in ogo]
        t_og = [[k.view(f"dog{gi}_{j}", ogo[gi] + j * 2048, ogo[gi] + (j + 1) * 2048) for j in range(4)]
                for gi in range(2)]

        phase_norm(n_idx)
        for i in range(4):
            k.op(DVE, lambda: nc.vector.memset(vst[i][:, :, 128:130], 1.0), writes=[t_vst[i]])

        def q_cons(c, hf, b):
            k.op(ACT, lambda: nc.scalar.copy(out=qT[:, c, hf * 512:(hf + 1) * 512], in_=bank[b][:]),
                 reads=[t_bank[b]], writes=[t_qT[c][hf], t_bank[b]])

        def k_cons(c, hf, b):
            i = c % 2
            k.op(DVE, lambda: nc.vector.tensor_copy(out=kst[i][:, hf * 512:(hf + 1) * 512], in_=bank[b][:]),
                 reads=[t_bank[b]], writes=[t_kst[i], t_bank[b]])
            if hf == 1:
                tk = Tl(f"kc{c}_{ti}", kst_sem[i]); kc_list[c].append(tk)
                k.dma(SP, tk, t_kst[i], [(kcache[c, :, ti * TT:(ti + 1) * TT], kst[i][:])])

        def v_cons(p, s, b):
            i = cnt["sg"] % 4; cnt["sg"] += 1
            k.op(ACT, lambda: nc.scalar.copy(out=vst[i][:, :, 0:128],
                                             in_=bank[b][:, 0:256].rearrange("p (h c) -> p h c", h=2)),
                 reads=[t_bank[b]], writes=[t_vst[i], t_bank[b]])
            blk = ti * NS + s
            tv = Tl(f"vc{p}_{blk}", vst_sem[i])
            vc_list[2 * p].append(tv); vc_list[2 * p + 1].append(tv)
            k.dma(SP, tv, t_vst[i], [(vcache[2 * p + hh, :, blk, :], vst[i][:, hh, :]) for hh in range(2)])

        proj_featmajor(diff_w_in_d, 0, 8, q_cons)
        proj_featmajor(diff_w_in_d, D, 8, k_cons)
        proj_tokmajor(diff_w_in_d, 2 * D, 4, v_cons)
        wo, t_wo = wout_load(diff_w_out_d)

        obk = [bank[4 + j].rearrange("p (m c) -> p m c", m=2) for j in range(4)]
        units = [(g, h) for g in range(2) for h in range(8)]
        itc = [0]
        itof = {}

        def geom(g):
            nk = ti * TT + (g + 1) * 512
            return nk, nk // 128, (ti * TT + g * 512) // 128

        def loads(ui):
            g, h = units[ui]
            li = ui % 2
            nk, nkb, qb0 = geom(g)
            k.dma(SP, t_kTh[li], list(kc_list[h]), [(kTh[li][:, 0:nk], kcache[h, :, 0:nk])])
            k.dma(SP, t_Vh[li], list(vc_list[h]), [(Vh[li][:, 0:nkb, :], vcache[h, :, 0:nkb, :])])

        def qk_exp(ui, kb):
            g, h = units[ui]
            li = ui % 2
            nk, nkb, qb0 = geom(g)
            itn = itc[0]; itc[0] += 1
            itof[(ui, kb)] = itn
            j0 = max(0, kb - qb0)
            b0 = 2 * (itn % 2)
            pi = itn % 3
            c0 = j0 * 128
            for m in range(2):
                bm = b0 + m
                lhs = kTh[li][m * 64:(m + 1) * 64, kb * 128:(kb + 1) * 128]
                k.op(PE, lambda: nc.tensor.matmul(bank[bm][:, c0:512], lhsT=lhs,
                                                  rhs=qT[m * 64:(m + 1) * 64, h, g * 512 + c0:(g + 1) * 512],
                                                  start=True, stop=True),
                     reads=[t_kTh[li], t_qT[h][g]], writes=[t_bank[bm]])
            k.op(ACT, lambda: nc.scalar.activation(out=pT[pi][:, :, c0:512],
                                                   in_=psall[:, b0:b0 + 2, c0:512], func=AF.Exp,
                                                   scale=0.125),
                 reads=[t_bank[b0], t_bank[b0 + 1]], writes=t_pT[pi][j0:4] + [t_bank[b0], t_bank[b0 + 1]])
            if kb >= qb0:
                k.op(DVE, lambda: nc.vector.tensor_tensor(out=pT[pi][:, :, c0:c0 + 128],
                                                          in0=pT[pi][:, :, c0:c0 + 128],
                                                          in1=maskTb[:].unsqueeze(1).to_broadcast([128, 2, 128]),
                                                          op=ALU.mult),
                     reads=[t_pT[pi][j0], t_maskTb], writes=[t_pT[pi][j0]])

        def pv(ui, kb):
            g, h = units[ui]
            li = ui % 2
            nk, nkb, qb0 = geom(g)
            j0 = max(0, kb - qb0)
            pi = itof[(ui, kb)] % 3
            if kb == 0:
                for j in range(4):
                    k.op(PE, lambda: nc.tensor.matmul(bank[4 + j][:, :], lhsT=zer[:, 0:128], rhs=zer[:, :],
                                                      start=True, stop=False),
                         reads=[t_zer], writes=[t_bank[4 + j]], inc=False)
            for j in range(3, j0 - 1, -1):
                last = (kb == qb0 + j)
                for m in range(2):
                    k.op(PE, lambda: nc.tensor.matmul(obk[j][:, m, 0:129],
                                                      lhsT=pT[pi][:, m, j * 128:(j + 1) * 128],
                                                      rhs=Vh[li][:, kb, 0:129],
                                                      start=False, stop=(last and m == 1)),
                         reads=[t_pT[pi][j], t_Vh[li]], writes=[t_bank[4 + j]], inc=(m == 1))

        def evac():
            for j in range(4):
                tb = t_bank[4 + j]
                k.op(DVE, lambda: nc.vector.tensor_copy(out=osb[:, j, :, 0:129], in_=obk[j][:, :, 0:129]),
                     reads=[tb], writes=[t_osb[j], tb])

        def finalize_a(ui):
            for j in range(4):
                to = t_osb[j]
                k.op(DVE, lambda: nc.vector.reciprocal(out=small[:, 16 + 2 * j:18 + 2 * j],
                                                       in_=osb[:, j, :, 128]),
                     reads=[to], writes=[t_small])
                k.op(DVE, lambda: nc.vector.tensor_tensor(out=small[:, 17 + 2 * j:18 + 2 * j],
                                                          in0=small[:, 17 + 2 * j:18 + 2 * j],
                                                          in1=lams[:, 4:5], op=ALU.mult),
                     reads=[t_small, t_lams], writes=[t_small])
                k.op(DVE, lambda: nc.vector.tensor_scalar(out=osb[:, j, 1, 0:128], in0=osb[:, j, 1, 0:128],
                                                          scalar1=small[:, 17 + 2 * j:18 + 2 * j], scalar2=None,
                                                          op0=ALU.mult),
                     reads=[to, t_small], writes=[to])
                k.op(DVE, lambda: nc.vector.scalar_tensor_tensor(out=osb[:, j, 0, 0:128],
                                                                 in0=osb[:, j, 0, 0:128],
                                                                 scalar=small[:, 16 + 2 * j:17 + 2 * j],
                                                                 in1=osb[:, j, 1, 0:128],
                                                                 op0=ALU.mult, op1=ALU.add),
                     reads=[to, t_small], writes=[to])

        def finalize_b(ui):
            g, h = units[ui]
            for j in range(4):
                to = t_osb[j]
                k.op(ACT, lambda: nc.scalar.activation(out=junk[:, 0:128], in_=osb[:, j, 0, 0:128],
                                                       func=AF.Square, accum_out=small[:, 32 + j:33 + j]),
                     reads=[to], writes=[t_junk, t_small])
            k.op(ACT, lambda: nc.scalar.activation(out=small[:, 36:40], in_=small[:, 32:36], func=AF.Ln,
                                                   bias=epsc[:, 0:1], scale=1.0 / 128),
                 reads=[t_small, t_epsc], writes=[t_small])
            k.op(ACT, lambda: nc.scalar.activation(out=small[:, 40:44], in_=small[:, 36:40], func=AF.Exp,
                                                   scale=-0.5),
                 reads=[t_small], writes=[t_small])
            for j in range(4):
                k.op(DVE, lambda: nc.vector.scalar_tensor_tensor(out=og[g % 2][:, j, h * 128:(h + 1) * 128],
                                                                 in0=osb[:, j, 0, 0:128],
                                                                 scalar=small[:, 40 + j:41 + j], in1=dgout[:],
                                                                 op0=ALU.mult, op1=ALU.mult),
                     reads=[t_osb[j], t_small, t_dgout], writes=[t_og[g % 2][j]])

        def group_transposes(g):
            for j in range(4):
                transpose_to_hT(lambda kc: og[g % 2][:, j, kc * 128:(kc + 1) * 128], t_og[g % 2][j], g * 4 + j,
                                use_bank=(j if g == 0 else None))

        loads(0)
        qk_exp(0, 0)
        for ui in range(len(units)):
            g, h = units[ui]
            nk, nkb, qb0 = geom(g)
            if ui + 1 < len(units):
                loads(ui + 1)
            for kb in range(nkb):
                if kb + 1 < nkb:
                    qk_exp(ui, kb + 1)
                pv(ui, kb)
                if ui > 0 and kb == 2:
                    finalize_b(ui - 1)
                if g == 1 and h == 0 and kb == 3:
                    group_transposes(0)
            evac()
            if ui + 1 < len(units):
                qk_exp(ui + 1, 0)
            finalize_a(ui)
        finalize_b(len(units) - 1)
        group_transposes(1)
        out_proj_fused(wo, t_wo, nxt)

    def final_store(tile_i, do_norm):
        outs = []
        if do_norm:
            gf = carve(0, 4096, F32); t_gf = k.view("gfin", 0, 4096, dsem=gf_sem)
            ob = [carve(4096 + i * 4096, 4096, F32) for i in range(2)]
            k.dma(SP, t_gf, None, [(gf[:], gfin_d)])
            if pre_done["v"]:
                pre_done["v"] = False
            else:
                rms_stats()
        for s in range(NS):
            r0 = tile_i * TT + s * 128
            if do_norm:
                i = s % 2
                t_ob = k.view(f"ob{i}", 4096 + i * 4096, 8192 + i * 4096, dsem=None)
                k.op(DVE, lambda: nc.vector.scalar_tensor_tensor(out=ob[i][:], in0=xt[:, s, :],
                                                                 scalar=rstd[:, s:s + 1], in1=gf[:],
                                                                 op0=ALU.mult, op1=ALU.mult),
                     reads=[t_xt[s], t_rstd[s], t_gf], writes=[t_ob])
                t_out = Tl(f"o{tile_i}_{s}", out_sems[i])
                k.dma(SP, t_out, t_ob, [(out_d[r0:r0 + 128, :], ob[i][:])])
            else:
                t_out = Tl(f"o{tile_i}_{s}", out_sems[s % 2])
                k.dma(SP, t_out, t_xt[s], [(out_d[r0:r0 + 128, :], xt[:, s, :])])
            outs.append(t_out)
        return outs

    for ti in range(ntiles):
        if ti == 0 or stages < 7:
            load_x(ti)
        if stages >= 1:
            ffn(0, 0, nxt=1 if stages >= 2 else None)
        if stages >= 2:
            gla(1, 2)
        if stages >= 3:
            ffn(1, 2, nxt=3 if stages >= 4 else None)
        if stages >= 4:
            ffn(2, 3, nxt=4 if stages >= 5 else None)
        if stages >= 5:
            diff(4, ti, 5)
        if stages >= 6:
            ffn(3, 5, nxt="final" if stages >= 7 else None, ti=ti, has_next=(ti + 1 < ntiles))
        if stages < 7:
            pre_done["v"] = False
            last_out += final_store(ti, False)
    for t in last_out[-4:]:
        k.wait_tile(SP, t)
    for key in out_sems:
        SP.h.wait_ge(k.semof[key], k.dcnt[key])
    print(f"[build] inst={k.ninst} waits={k.nwaits} sems={k.nsem}")
    stack.close()
    return nc


def _consts():
    c = {}
    c["ident"] = np.eye(128, dtype=np.float32).astype(ml_dtypes.bfloat16)
    s = np.arange(128)[:, None]
    t = np.arange(128)[None, :]
    c["tri"] = np.where(s <= t, -1.0 / 16.0, 0.0).astype(np.float32)
    c["maskT"] = np.where(s <= t, 1.0, 0.0).astype(np.float32)
    c["negmask"] = np.where(s > t, NEG, 0.0).astype(np.float32).astype(ml_dtypes.bfloat16)
    return c


def make_in_maps(inputs, n_cores=8):
    f = lambda n: np.ascontiguousarray(np.asarray(inputs[n], np.float32))
    x = f("x")
    shared = dict(_consts())
    norm_names = ["l0_norm_ffn1", "l0_norm_mix", "l0_norm_ffn2", "l1_norm_ffn1", "l1_norm_mix", "l1_norm_ffn2",
                  "final_norm"]
    norms = np.stack([f(n).reshape(KC, 128).T for n in norm_names], axis=1)
    shared["norms"] = np.ascontiguousarray(norms)
    ffn_names = [("l0_ffn1_w_in", "l0_ffn1_w_down"), ("l0_ffn2_w_in", "l0_ffn2_w_down"),
                 ("l1_ffn1_w_in", "l1_ffn1_w_down"), ("l1_ffn2_w_in", "l1_ffn2_w_down")]
    for i, (a, b) in enumerate(ffn_names):
        shared[f"ffn{i}_w_in"] = f(a)
        shared[f"ffn{i}_w_down"] = f(b)
    shared["gla_w_in"] = f("l0_gla_w_in")
    shared["gla_wg2b"] = np.ascontiguousarray(np.concatenate([f("l0_gla_w_gate2"), f("l0_gla_b_gate2")[None, :]], 0))
    shared["gla_gout"] = np.ascontiguousarray(np.broadcast_to(f("l0_gla_norm_out")[None, :], (128, 256)))
    shared["gla_w_out"] = f("l0_gla_w_out")
    shared["diff_w_in"] = f("l1_diff_w_in")
    lam = np.stack([f("l1_diff_lambda_q1"), f("l1_diff_lambda_k1"), f("l1_diff_lambda_q2"), f("l1_diff_lambda_k2")])
    shared["diff_lam"] = np.ascontiguousarray(np.broadcast_to(lam[None], (128, 4, 64)))
    shared["diff_gout"] = np.ascontiguousarray(np.broadcast_to(f("l1_diff_norm_out")[None, :], (128, 128)))
    shared["diff_w_out"] = f("l1_diff_w_out")
    shared["gfin"] = np.ascontiguousarray(np.broadcast_to(f("final_norm")[None, :], (128, D)))
    maps = []
    for ci in range(n_cores):
        m = dict(shared)
        m["x"] = np.ascontiguousarray(x[ci])
        maps.append(m)
    return maps


def kernel(**inputs):
    nc = build_nc()
    in_maps = make_in_maps(inputs, 8)
    res = run_bass_kernel_spmd(nc, in_maps, core_ids=list(range(8)))
    return np.stack([np.asarray(r["out"], dtype=np.float32) for r in res.results], axis=0)
```

```python
import numpy as np
import ml_dtypes
from contextlib import ExitStack
import concourse.bass as bass
import concourse.mybir as mybir
from concourse.bass_utils import run_bass_kernel_spmd

F32 = mybir.dt.float32
BF16 = mybir.dt.bfloat16
ALU = mybir.AluOpType
AF = mybir.ActivationFunctionType

D = 1024
SEQ = 4096
DFF = 2816
NHC = DFF // 128
KC = D // 128
TT = 1024
NS = TT // 128
NT = SEQ // TT
EPS = 1e-6
GLA_DK, GLA_DV, GLA_R = 512, 1024, 16
GLA_IN = 2 * GLA_DK + 2 * GLA_DV + GLA_R
LAMBDA_INIT = 0.8 - 0.6 * float(np.exp(-0.3))
SEM_EPOCH = 30000


class Eng:
    def __init__(self, k, name, h, ordered=False):
        self.k, self.name, self.h, self.ordered = k, name, h, ordered
        self.epoch = -1
        self.count = 0
        self.known = {}
        self.hist = {}
        self.new_epoch()

    def new_epoch(self):
        self.epoch += 1
        self.sem = self.k.new_sem(f"{self.name}{self.epoch}")
        self.key = (self.name, self.epoch)
        self.k.semof[self.key] = self.sem
        self.count = 0
        self.pending = False


class Tl:
    def __init__(self, name, dsem=None):
        self.name = name
        self.w = None
        self.r = []
        self.dsem = dsem
        self.dcount = 0


class K:
    def __init__(self, nc, stack):
        self.nc, self.stack = nc, stack
        self.semof = {}
        self.nsem = 0
        self.pe = Eng(self, "pe", nc.tensor, ordered=True)
        self.act = Eng(self, "act", nc.scalar)
        self.dve = Eng(self, "dve", nc.vector)
        self.pool = Eng(self, "pool", nc.gpsimd)
        self.sp = Eng(self, "sp", nc.sync)
        self.nwaits = 0
        self.ninst = 0
        self.dcnt = {}
        self.live = []

    def new_sem(self, name):
        self.nsem += 1
        return self.stack.enter_context(self.nc.semaphore(f"s_{name}_{self.nsem}"))

    def dsem(self, name):
        key = ("dma", name, self.nsem)
        self.semof[key] = self.new_sem("d" + name)
        self.dcnt[key] = 0
        return key

    def tile(self, name, dma=False):
        t = Tl(name)
        if dma:
            t.dsem = self.dsem(name)
        return t

    def view(self, name, lo, hi, dsem=None):
        t = Tl(name, dsem)
        t.lo, t.hi = lo, hi
        pend = {}
        for o in list(self.live):
            if o.lo < hi and lo < o.hi:
                for d in ([o.w] if o.w else []) + o.r:
                    if pend.get(d[0], 0) < d[1]:
                        pend[d[0]] = d[1]
                self.live.remove(o)
                o.dead = True
        t.r = list(pend.items())
        self.live.append(t)
        return t

    def _deps(self, eng, reads, writes):
        deps = {}
        for t in list(reads) + list(writes):
            if getattr(t, "dead", False):
                raise RuntimeError(f"use of retired view {t.name}")
        def add(d):
            if d is None:
                return
            key, c = d
            if deps.get(key, 0) < c:
                deps[key] = c
        for t in reads:
            add(t.w)
        for t in writes:
            add(t.w)
            for d in t.r:
                add(d)
        return deps

    def _wait(self, eng, deps):
        for key, c in deps.items():
            if eng.ordered and key[0] == eng.name:
                continue
            if eng.known.get(key, 0) >= c:
                continue
            if key[0] != "dma":
                src = getattr(self, key[0])
                if key == src.key and c > src.count:
                    raise RuntimeError(f"wait on unissued inc: {eng.name} waits {key} {c} > {src.count}")
            eng.h.wait_ge(self.semof[key], c)
            self.nwaits += 1
            eng.known[key] = c
            hk = self.hist_lookup(key, c)
            if hk:
                for k2, c2 in hk.items():
                    if eng.known.get(k2, 0) < c2:
                        eng.known[k2] = c2

    def hist_lookup(self, key, c):
        if key[0] == "dma":
            return None
        src = getattr(self, key[0])
        return src.hist.get((key, c))

    def op(self, eng, fn, reads=(), writes=(), inc=True):
        deps = self._deps(eng, reads, writes)
        self._wait(eng, deps)
        ins = fn()
        self.ninst += 1
        if inc:
            if eng.count >= SEM_EPOCH and not eng.pending:
                eng.new_epoch()
            eng.count += 1
            ins.then_inc(eng.sem, 1)
            my = (eng.key, eng.count)
            eng.hist[my] = dict(eng.known)
            eng.pending = False
        else:
            my = (eng.key, eng.count + 1)
            eng.pending = True
        for t in reads:
            t.r.append(my)
        for t in writes:
            t.w = my
            t.r = []
        return ins

    def dma(self, q, dst, src, pairs):
        assert dst.dsem is not None
        srcs = [] if src is None else (list(src) if isinstance(src, (list, tuple)) else [src])
        deps = self._deps(q, srcs, [dst])
        self._wait(q, deps)
        sem = self.semof[dst.dsem]
        for o, i in pairs:
            q.h.dma_start(out=o, in_=i).then_inc(sem, 16)
            self.dcnt[dst.dsem] += 16
        my = (dst.dsem, self.dcnt[dst.dsem])
        for t in srcs:
            t.r.append(my)
        dst.w = my
        dst.r = []

    def wait_tile(self, eng, t):
        self._wait(eng, self._deps(eng, [t], []))


A_BYTES = NHC * TT * 2
W_BYTES = NHC * D * 2
B_BYTES = 32768
WORK_BYTES = A_BYTES + W_BYTES + B_BYTES
NEG = -30000.0


def build_nc(stages=99, ntiles=NT):
    nc = bass.Bass("TRN2", target_bir_lowering=False)
    stack = ExitStack()
    k = K(nc, stack)
    PE, ACT, DVE, POOL, SP = k.pe, k.act, k.dve, k.pool, k.sp

    def din(name, shape, dt=F32):
        return nc.dram_tensor(name, list(shape), dt, kind="ExternalInput").ap()

    x_d = din("x", [SEQ, D])
    out_d = nc.dram_tensor("out", [SEQ, D], F32, kind="ExternalOutput").ap()
    ident_d = din("ident", [128, 128], BF16)
    norms_d = din("norms", [128, 7, KC])
    ffn_w_in = [din(f"ffn{i}_w_in", [D, 2 * DFF]) for i in range(4)]
    ffn_w_down = [din(f"ffn{i}_w_down", [DFF, D]) for i in range(4)]
    gla_w_in_d = din("gla_w_in", [D, GLA_IN])
    gla_wg2b_d = din("gla_wg2b", [17, GLA_DK])
    gla_gout_d = din("gla_gout", [128, 256])
    gla_w_out_d = din("gla_w_out", [D, D])
    diff_w_in_d = din("diff_w_in", [D, 3 * D])
    diff_lam_d = din("diff_lam", [128, 4, 64])
    diff_gout_d = din("diff_gout", [128, 128])
    diff_w_out_d = din("diff_w_out", [D, D])
    tri_d = din("tri", [128, 128])
    maskT_d = din("maskT", [128, 128])
    negmask_d = din("negmask", [128, 128], BF16)
    gfin_d = din("gfin", [128, D])
    kcache = nc.dram_tensor("kcache", [8, 128, SEQ], BF16).ap()
    vcache = nc.dram_tensor("vcache", [8, 128, SEQ // 128, 130], BF16).ap()

    def sb(name, shape, dt):
        return stack.enter_context(nc.sbuf_tensor("sb_" + name, list(shape), dt))

    ident = sb("ident", [128, 128], BF16); t_ident = k.tile("ident", dma=True)
    norms = sb("norms", [128, 7, KC], F32); t_norms = k.tile("norms", dma=True)
    xt = sb("xt", [128, NS, D], F32)
    t_xt = [k.tile(f"xt{s}", dma=True) for s in range(NS)]
    hT = sb("hT", [128, KC, TT], BF16)
    t_hT = [k.tile(f"hT{s}") for s in range(NS)]
    NRING = 3
    wring = [sb(f"wring{i}", [128, KC, 256], BF16) for i in range(NRING)]
    t_wring = [k.tile(f"wring{i}", dma=True) for i in range(NRING)]
    ring_pos = [0]
    hb = [sb(f"hb{i}", [128, D], BF16) for i in range(2)]
    t_hb = [k.tile(f"hb{i}") for i in range(2)]
    junk = sb("junk", [128, D], BF16); t_junk = k.tile("junk")
    sg = [sb(f"sg{i}", [128, 512], F32) for i in range(2)]
    t_sg = [k.tile(f"sg{i}") for i in range(2)]
    ssq = sb("ssq", [128, 3 * NS], F32); t_ssq = [k.tile(f"ssq{s}") for s in range(NS)]
    rstd = sb("rstd", [128, NS], F32); t_rstd = [k.tile(f"rstd{s}") for s in range(NS)]
    epsc = sb("epsc", [128, 2], F32); t_epsc = k.tile("epsc")
    tri = sb("tri", [128, 128], F32); t_tri = k.tile("tri", dma=True)
    maskT = sb("maskT", [128, 128], F32); t_maskT = k.tile("maskT", dma=True)
    negmask = sb("negmask", [128, 128], BF16); t_negmask = k.tile("negmask", dma=True)
    maskTb = sb("maskTb", [128, 128], BF16); t_maskTb = k.tile("maskTb")
    zer = sb("zer", [128, 512], BF16); t_zer = k.tile("zer")
    wg2b = sb("wg2b", [17, GLA_DK], BF16); t_wg2b = k.tile("wg2b", dma=True)
    trib = sb("trib", [128, 128], BF16); t_trib = k.tile("trib")
    ggout = sb("ggout", [128, 256], F32); t_ggout = k.tile("ggout", dma=True)
    dgout = sb("dgout", [128, 128], F32); t_dgout = k.tile("dgout", dma=True)
    lamv = sb("lamv", [128, 4, 64], F32); t_lamv = k.tile("lamv", dma=True)
    lams = sb("lams", [128, 8], F32); t_lams = k.tile("lams")
    Sst = sb("Sst", [128, 4, 256], F32); t_S = [k.tile(f"S{h}") for h in range(4)]
    Sbf = sb("Sbf", [128, 4, 256], BF16); t_Sbf = [k.tile(f"Sbf{h}") for h in range(4)]
    small = sb("small", [128, 64], F32); t_small = k.tile("small")
    rem = nc.sbuf_bytes_remaining
    print(f"[build] sbuf remaining before work: {rem}, work={WORK_BYTES}")
    work = sb("work", [128, WORK_BYTES // 2], BF16)

    def carve(lo, nbytes, dt, pattern=None, **kw):
        ap = work[:, lo // 2:(lo + nbytes) // 2]
        if dt == F32:
            ap = ap.bitcast(F32)
        if pattern:
            ap = ap.rearrange(pattern, **kw)
        return ap

    psall = stack.enter_context(nc.psum_tensor("ps_all", [128, 8, 512], F32))
    bank = [psall[:, i, :] for i in range(8)]
    t_bank = [k.tile(f"bank{i}") for i in range(8)]

    def bank_bf(i, pattern=None, **kw):
        ap = bank[i].bitcast(BF16)
        if pattern:
            ap = ap.rearrange(pattern, **kw)
        return ap

    cnt = {"gu": 0, "o": 0, "t": 0, "hb": 0, "sg": 0, "fm": 0, "tm": 0}

    k.op(DVE, lambda: nc.vector.memset(epsc[:, 0:1], EPS), writes=[t_epsc])
    k.op(DVE, lambda: nc.vector.memset(epsc[:, 1:2], 1.0), writes=[t_epsc])
    k.dma(SP, t_ident, None, [(ident[:], ident_d)])
    k.dma(SP, t_norms, None, [(norms[:], norms_d)])
    k.dma(SP, t_tri, None, [(tri[:], tri_d)])
    k.dma(SP, t_maskT, None, [(maskT[:], maskT_d)])
    k.dma(SP, t_negmask, None, [(negmask[:], negmask_d)])
    k.op(DVE, lambda: nc.vector.tensor_copy(out=maskTb[:], in_=maskT[:]), reads=[t_maskT], writes=[t_maskTb])
    k.op(DVE, lambda: nc.vector.memset(zer[:], 0.0), writes=[t_zer])
    k.dma(POOL, t_wg2b, None, [(wg2b[:], gla_wg2b_d)])
    k.op(DVE, lambda: nc.vector.tensor_copy(out=trib[:], in_=tri[:]), reads=[t_tri], writes=[t_trib])
    k.dma(SP, t_ggout, None, [(ggout[:], gla_gout_d)])
    k.dma(SP, t_dgout, None, [(dgout[:], diff_gout_d)])
    k.dma(SP, t_lamv, None, [(lamv[:], diff_lam_d)])
    for h in range(4):
        k.op(DVE, lambda: nc.vector.memset(Sst[:, h, :], 0.0), writes=[t_S[h]])
        k.op(DVE, lambda: nc.vector.memset(Sbf[:, h, :], 0.0), writes=[t_Sbf[h]])
    k.op(DVE, lambda: nc.vector.tensor_scalar(out=dgout[:], in0=dgout[:], scalar1=1.0 - LAMBDA_INIT, scalar2=None,
                                              op0=ALU.mult), reads=[t_dgout], writes=[t_dgout])
    k.op(DVE, lambda: nc.vector.tensor_tensor(out=lamv[:, 0, :], in0=lamv[:, 0, :], in1=lamv[:, 1, :], op=ALU.mult),
         reads=[t_lamv], writes=[t_lamv])
    k.op(DVE, lambda: nc.vector.tensor_tensor(out=lamv[:, 2, :], in0=lamv[:, 2, :], in1=lamv[:, 3, :], op=ALU.mult),
         reads=[t_lamv], writes=[t_lamv])
    k.op(DVE, lambda: nc.vector.reduce_sum(out=lams[:, 0:1], in_=lamv[:, 0, :], axis=mybir.AxisListType.X),
         reads=[t_lamv], writes=[t_lams])
    k.op(DVE, lambda: nc.vector.reduce_sum(out=lams[:, 1:2], in_=lamv[:, 2, :], axis=mybir.AxisListType.X),
         reads=[t_lamv], writes=[t_lams])
    k.op(ACT, lambda: nc.scalar.activation(out=lams[:, 2:4], in_=lams[:, 0:2], func=AF.Exp),
         reads=[t_lams], writes=[t_lams])
    k.op(DVE, lambda: nc.vector.scalar_tensor_tensor(out=lams[:, 4:5], in0=lams[:, 3:4], scalar=-LAMBDA_INIT,
                                                     in1=lams[:, 2:3], op0=ALU.add, op1=ALU.subtract),
         reads=[t_lams], writes=[t_lams])

    def load_x(tile_i):
        for s in range(NS):
            r0 = tile_i * TT + s * 128
            k.dma(SP, t_xt[s], None, [(xt[:, s, :], x_d[r0:r0 + 128, :])])

    def norm_sub_stats(s):
        k.op(ACT, lambda: nc.scalar.activation(out=junk[:], in_=xt[:, s, :], func=AF.Square,
                                               accum_out=ssq[:, s:s + 1]),
             reads=[t_xt[s]], writes=[t_junk, t_ssq[s]])
        k.op(ACT, lambda: nc.scalar.activation(out=ssq[:, NS + s:NS + s + 1], in_=ssq[:, s:s + 1], func=AF.Ln,
                                               bias=epsc[:, 0:1], scale=1.0 / D),
             reads=[t_ssq[s], t_epsc], writes=[t_ssq[s]])
        k.op(ACT, lambda: nc.scalar.activation(out=rstd[:, s:s + 1], in_=ssq[:, NS + s:NS + s + 1], func=AF.Exp,
                                               scale=-0.5),
             reads=[t_ssq[s]], writes=[t_rstd[s]])

    def rms_stats():
        for s in range(NS):
            norm_sub_stats(s)

    def norm_sub_hb(s, on_act=False):
        i_hb = cnt["hb"] % 2; cnt["hb"] += 1
        if on_act:
            k.op(ACT, lambda: nc.scalar.activation(out=hb[i_hb][:], in_=xt[:, s, :], func=AF.Copy,
                                                   scale=rstd[:, s:s + 1]),
                 reads=[t_rstd[s], t_xt[s]], writes=[t_hb[i_hb]])
        else:
            k.op(DVE, lambda: nc.vector.tensor_scalar(out=hb[i_hb][:], in0=xt[:, s, :], scalar1=rstd[:, s:s + 1],
                                                      scalar2=None, op0=ALU.mult),
                 reads=[t_rstd[s], t_xt[s]], writes=[t_hb[i_hb]])
        return i_hb

    def norm_sub_tr(s, i_hb, n_idx):
        transpose_to_hT(lambda kc: hb[i_hb][:, kc * 128:(kc + 1) * 128], t_hb[i_hb], s, gscale_idx=n_idx)

    def transpose_to_hT(src_ap_fn, src_tile, s, gscale_idx=None, use_bank=None):
        if use_bank is None:
            i_t = 6 + cnt["t"] % 2; cnt["t"] += 1
        else:
            i_t = use_bank
        pt = bank_bf(i_t, "p (a b) -> p a b", a=KC)
        for kc in range(KC):
            k.op(PE, lambda: nc.tensor.transpose(pt[:, kc, :], src_ap_fn(kc), ident[:]),
                 reads=[src_tile, t_ident], writes=[t_bank[i_t]], inc=(kc == KC - 1))
        if gscale_idx is None:
            k.op(ACT, lambda: nc.scalar.copy(out=hT[:, :, s * 128:(s + 1) * 128], in_=pt[:, :, :]),
                 reads=[t_bank[i_t]], writes=[t_hT[s], t_bank[i_t]])
        else:
            k.op(DVE, lambda: nc.vector.tensor_tensor(out=hT[:, :, s * 128:(s + 1) * 128], in0=pt[:, :, :],
                                                      in1=norms[:, gscale_idx, :].unsqueeze(2).to_broadcast(
                                                          [128, KC, 128]),
                                                      op=ALU.mult),
                 reads=[t_bank[i_t], t_norms], writes=[t_hT[s], t_bank[i_t]])

    def norm_to_hT(n_idx):
        pend = None
        for s in range(NS):
            norm_sub_stats(s)
            i_hb = norm_sub_hb(s)
            if pend is not None:
                norm_sub_tr(*pend)
            pend = (s, i_hb, n_idx)
        norm_sub_tr(*pend)

    pre_done = {"v": False}

    def phase_norm(n_idx):
        if pre_done["v"]:
            pre_done["v"] = False
        else:
            norm_to_hT(n_idx)

    def ring_load(w_d, col_specs):
        i = ring_pos[0] % NRING; ring_pos[0] += 1
        pairs = []
        for (d0, s0, n) in col_specs:
            pairs.append((wring[i][:, :, d0:d0 + n],
                          w_d[:, s0:s0 + n].rearrange("(kc p) c -> p kc c", p=128)))
        k.dma(POOL, t_wring[i], None, pairs)
        return i

    def proj_featmajor(w_d, col0, nchunks, consumer, halves=(0, 1), banks=(0, 1, 2, 3)):
        for c0 in range(0, nchunks, 2):
            ncol = min(2, nchunks - c0) * 128
            i = ring_load(w_d, [(0, col0 + c0 * 128, ncol)])
            for cc in range(ncol // 128):
                for hf in halves:
                    b = banks[cnt["fm"] % len(banks)]; cnt["fm"] += 1
                    for kc in range(KC):
                        k.op(PE, lambda: nc.tensor.matmul(bank[b][:], lhsT=wring[i][:, kc, cc * 128:(cc + 1) * 128],
                                                          rhs=hT[:, kc, hf * 512:(hf + 1) * 512],
                                                          start=(kc == 0), stop=(kc == KC - 1)),
                             reads=[t_wring[i]] + t_hT[hf * 4:(hf + 1) * 4], writes=[t_bank[b]],
                             inc=(kc == KC - 1))
                    consumer(c0 + cc, hf, b)

    def proj_tokmajor(w_d, col0, npanels, consumer, banks=(4, 5)):
        for p in range(npanels):
            i = ring_load(w_d, [(0, col0 + p * 256, 256)])
            for s in range(NS):
                b = banks[cnt["tm"] % len(banks)]; cnt["tm"] += 1
                for kc in range(KC):
                    k.op(PE, lambda: nc.tensor.matmul(bank[b][:, 0:256], lhsT=hT[:, kc, s * 128:(s + 1) * 128],
                                                      rhs=wring[i][:, kc, :],
                                                      start=(kc == 0), stop=(kc == KC - 1)),
                         reads=[t_wring[i], t_hT[s]], writes=[t_bank[b]], inc=(kc == KC - 1))
                consumer(p, s, b)

    def resid_add_consumer(p, s, b):
        k.op(DVE, lambda: nc.vector.tensor_tensor(out=xt[:, s, p * 256:(p + 1) * 256], in0=bank[b][:, 0:256],
                                                  in1=xt[:, s, p * 256:(p + 1) * 256], op=ALU.add),
             reads=[t_bank[b], t_xt[s]], writes=[t_xt[s], t_bank[b]])

    wdn_sems = [k.dsem(f"wdn{c}") for c in range(NHC)]
    out_sems = [k.dsem(f"out{i}") for i in range(2)]
    gf_sem = k.dsem("gfin")
    last_out = []

    def ffn(fi, n_idx, nxt=None, ti=None, has_next=False):
        w_in, w_dn = ffn_w_in[fi], ffn_w_down[fi]
        act = carve(0, A_BYTES, BF16, "p (c t) -> p c t", c=NHC)
        t_act = [[k.view(f"act{c}_{h}", (c * TT + h * 512) * 2, (c * TT + h * 512 + 512) * 2) for h in range(2)]
                 for c in range(NHC)]
        wdn = carve(A_BYTES, W_BYTES, BF16, "p (c t) -> p c t", c=NHC)
        t_wdn = [k.view(f"wdn{c}", A_BYTES + c * D * 2, A_BYTES + (c + 1) * D * 2, dsem=wdn_sems[c])
                 for c in range(NHC)]
        phase_norm(n_idx)
        PRE = 2
        slots = {}
        for c in range(min(PRE, NHC)):
            slots[c] = ring_load(w_in, [(0, c * 128, 128), (128, DFF + c * 128, 128)])
        for c in range(NHC):
            if c + PRE < NHC:
                slots[c + PRE] = ring_load(w_in, [(0, (c + PRE) * 128, 128), (128, DFF + (c + PRE) * 128, 128)])
            k.dma(POOL, t_wdn[c], None, [(wdn[:, c, :], w_dn[c * 128:(c + 1) * 128, :])])
            i = slots[c]
            for hf in range(2):
                j = cnt["gu"] % 2; cnt["gu"] += 1
                bg, bu = j, 2 + j
                hts = t_hT[hf * 4:(hf + 1) * 4]
                for kc in range(KC):
                    k.op(PE, lambda: nc.tensor.matmul(bank[bg][:], lhsT=wring[i][:, kc, 0:128],
                                                      rhs=hT[:, kc, hf * 512:(hf + 1) * 512],
                                                      start=(kc == 0), stop=(kc == KC - 1)),
                         reads=[t_wring[i]] + hts, writes=[t_bank[bg]], inc=(kc == KC - 1))
                for kc in range(KC):
                    k.op(PE, lambda: nc.tensor.matmul(bank[bu][:], lhsT=wring[i][:, kc, 128:256],
                                                      rhs=hT[:, kc, hf * 512:(hf + 1) * 512],
                                                      start=(kc == 0), stop=(kc == KC - 1)),
                         reads=[t_wring[i]] + hts, writes=[t_bank[bu]], inc=(kc == KC - 1))
                q = cnt["sg"] % 2; cnt["sg"] += 1
                k.op(ACT, lambda: nc.scalar.activation(out=sg[q][:], in_=bank[bg][:], func=AF.Silu),
                     reads=[t_bank[bg]], writes=[t_sg[q], t_bank[bg]])
                k.op(DVE, lambda: nc.vector.tensor_tensor(out=act[:, c, hf * 512:(hf + 1) * 512],
                                                          in0=bank[bu][:], in1=sg[q][:], op=ALU.mult),
                     reads=[t_sg[q], t_bank[bu]], writes=[t_act[c][hf], t_bank[bu]])
        pend = None
        fin = {}
        if nxt == "final":
            B0f = A_BYTES + W_BYTES
            gf = carve(B0f, 4096, F32); t_gf = k.view("gfin", B0f, B0f + 4096, dsem=gf_sem)
            obf = [carve(B0f + 4096 + i * 4096, 4096, F32) for i in range(2)]
            k.dma(SP, t_gf, None, [(gf[:], gfin_d)])

        def fin_step(step):
            a, b, c = step - 1, step - 2, step - 3
            if 0 <= a < NS:
                norm_sub_stats(a)
                i = a % 2
                t_ob = k.view(f"ob{i}", B0f + 4096 + i * 4096, B0f + 8192 + i * 4096, dsem=None)
                k.op(DVE, lambda: nc.vector.scalar_tensor_tensor(out=obf[i][:], in0=xt[:, a, :],
                                                                 scalar=rstd[:, a:a + 1], in1=gf[:],
                                                                 op0=ALU.mult, op1=ALU.mult),
                     reads=[t_xt[a], t_rstd[a], t_gf], writes=[t_ob])
                r0 = ti * TT + a * 128
                t_out = Tl(f"o{ti}_{a}", out_sems[i])
                k.dma(SP, t_out, t_ob, [(out_d[r0:r0 + 128, :], obf[i][:])])
                last_out.append(t_out)
                if has_next:
                    r1 = (ti + 1) * TT + a * 128
                    k.dma(SP, t_xt[a], None, [(xt[:, a, :], x_d[r1:r1 + 128, :])])
            if has_next and 0 <= b < NS:
                norm_sub_stats(b)
                fin[b] = norm_sub_hb(b)
            if has_next and 0 <= c < NS:
                norm_sub_tr(c, fin[c], 0)

        for s in range(NS):
            if nxt == "final":
                fin_step(s)
            elif s > 0 and nxt is not None:
                norm_sub_stats(s - 1)
                if pend is not None:
                    norm_sub_tr(*pend)
                pend = (s - 1, norm_sub_hb(s - 1), nxt)
            for of in range(2):
                j = 4 + cnt["o"] % 2; cnt["o"] += 1
                for c in range(NHC):
                    k.op(PE, lambda: nc.tensor.matmul(bank[j][:], lhsT=act[:, c, s * 128:(s + 1) * 128],
                                                      rhs=wdn[:, c, of * 512:(of + 1) * 512],
                                                      start=(c == 0), stop=(c == NHC - 1)),
                         reads=[t_act[c][s // 4], t_wdn[c]], writes=[t_bank[j]], inc=(c == NHC - 1))
                k.op(DVE, lambda: nc.vector.scalar_tensor_tensor(out=xt[:, s, of * 512:(of + 1) * 512],
                                                                 in0=bank[j][:], scalar=0.5,
                                                                 in1=xt[:, s, of * 512:(of + 1) * 512],
                                                                 op0=ALU.mult, op1=ALU.add),
                     reads=[t_bank[j], t_xt[s]], writes=[t_xt[s], t_bank[j]])

        if nxt == "final":
            for st in range(NS, NS + 3):
                fin_step(st)
            pre_done["v"] = has_next
        elif nxt is not None:
            norm_sub_stats(NS - 1)
            if pend is not None:
                norm_sub_tr(*pend)
            norm_sub_tr(NS - 1, norm_sub_hb(NS - 1), nxt)
            pre_done["v"] = True

    wo_sems = [k.dsem(f"wo{p}") for p in range(4)]

    def wout_load(w_d):
        wo = [carve(A_BYTES + p * 4096, 4096, BF16, "p (kc c) -> p kc c", kc=KC) for p in range(4)]
        t_wo = [k.view(f"wo{p}", A_BYTES + p * 4096, A_BYTES + (p + 1) * 4096, dsem=wo_sems[p]) for p in range(4)]
        for p in range(4):
            k.dma(POOL, t_wo[p], None,
                  [(wo[p][:, :, :], w_d[:, p * 256:(p + 1) * 256].rearrange("(kc p) c -> p kc c", p=128))])
        return wo, t_wo

    def out_proj_fused(wo, t_wo, nxt):
        pend = None
        for s in range(NS):
            for p in range(4):
                b = 4 + cnt["tm"] % 2; cnt["tm"] += 1
                for kc in range(KC):
                    k.op(PE, lambda: nc.tensor.matmul(bank[b][:, 0:256], lhsT=hT[:, kc, s * 128:(s + 1) * 128],
                                                      rhs=wo[p][:, kc, :], start=(kc == 0), stop=(kc == KC - 1)),
                         reads=[t_wo[p], t_hT[s]], writes=[t_bank[b]], inc=(kc == KC - 1))
                resid_add_consumer(p, s, b)
            norm_sub_stats(s)
            if pend is not None:
                norm_sub_tr(*pend)
            pend = (s, norm_sub_hb(s, on_act=True), nxt)
        norm_sub_tr(*pend)
        pre_done["v"] = True

    def gla(n_idx, nxt):
        B0 = A_BYTES + W_BYTES
        Eb = carve(0, 16384, F32, "p (h t) -> p h t", h=4)
        t_Eb = [k.view(f"Eb{s}", s * 2048, (s + 1) * 2048) for s in range(NS)]
        Ei = carve(16384, 16384, F32, "p (h t) -> p h t", h=4)
        t_Ei = [k.view(f"Ei{s}", 16384 + s * 2048, 16384 + (s + 1) * 2048) for s in range(NS)]
        qd = carve(32768, 8192, BF16, "p (h t) -> p h t", h=4)
        t_qd = [[k.view(f"qd{h}_{f}", 32768 + (h * TT + f * 512) * 2, 32768 + (h * TT + f * 512 + 512) * 2)
                 for f in range(2)] for h in range(4)]
        glr = carve(40960, 2048, BF16); t_glr = k.view("glr", 40960, 43008)
        sph = [carve(43008 + i * 1024, 1024, BF16) for i in range(2)]
        t_sph = [k.view(f"sph{i}", 43008 + i * 1024, 43008 + (i + 1) * 1024) for i in range(2)]
        ki = carve(B0, 8192, BF16, "p (h t) -> p h t", h=4)
        t_ki = [[k.view(f"ki{h}_{f}", B0 + (h * TT + f * 512) * 2, B0 + (h * TT + f * 512 + 512) * 2)
                 for f in range(2)] for h in range(4)]
        kt = carve(B0 + 8192, 8192, BF16, "p (s d) -> p s d", s=NS)
        t_kt = [k.view(f"kt{s}", B0 + 8192 + s * 1024, B0 + 8192 + (s + 1) * 1024) for s in range(NS)]
        spb = [carve(B0 + 16384 + i * 2048, 2048, F32) for i in range(2)]
        t_sp = [k.view(f"sp{i}", B0 + 16384 + i * 2048, B0 + 16384 + (i + 1) * 2048) for i in range(2)]
        og = [carve(B0 + 20480 + i * 2048, 2048, BF16) for i in range(2)]
        t_og = [k.view(f"og{i}", B0 + 20480 + i * 2048, B0 + 20480 + (i + 1) * 2048) for i in range(2)]
        attm = [carve(B0 + 24576 + i * 1024, 1024, BF16, "p (h t) -> p h t", h=4) for i in range(2)]
        t_attm = [k.view(f"attm{i}", B0 + 24576 + i * 1024, B0 + 24576 + (i + 1) * 1024) for i in range(2)]
        dec = carve(B0 + 26624, 128, F32, "p (s h) -> p s h", s=NS)
        t_dec = [k.view(f"dec{s}", B0 + 26624 + s * 16, B0 + 26624 + (s + 1) * 16) for s in range(NS)]
        spl = [carve(B0 + 29184 + i * 1024, 1024, BF16) for i in range(2)]
        t_spl = [k.view(f"spl{i}", B0 + 29184 + i * 1024, B0 + 29184 + (i + 1) * 1024) for i in range(2)]
        otmp = [carve(B0 + 27136 + i * 1024, 1024, F32) for i in range(2)]
        t_otmp = [k.view(f"otmp{i}", B0 + 27136 + i * 1024, B0 + 27136 + (i + 1) * 1024) for i in range(2)]

        phase_norm(n_idx)
        k.op(DVE, lambda: nc.vector.memset(glr[0:17, :], 1.0), writes=[t_glr])

        i = ring_load(gla_w_in_d, [(0, 2 * GLA_DK + 2 * GLA_DV, GLA_R)])
        for hf in range(2):
            b = hf
            for kc in range(KC):
                k.op(PE, lambda: nc.tensor.matmul(bank[b][0:16, :], lhsT=wring[i][:, kc, 0:16],
                                                  rhs=hT[:, kc, hf * 512:(hf + 1) * 512],
                                                  start=(kc == 0), stop=(kc == KC - 1)),
                     reads=[t_wring[i]] + t_hT[hf * 4:(hf + 1) * 4], writes=[t_bank[b]], inc=(kc == KC - 1))
            k.op(DVE, lambda: nc.vector.tensor_copy(out=glr[0:16, hf * 512:(hf + 1) * 512], in_=bank[b][0:16, :]),
                 reads=[t_bank[b]], writes=[t_glr, t_bank[b]])
        def gate_A(s):
            bz = 2 + s % 2
            q = s % 2
            k.op(PE, lambda: nc.tensor.matmul(bank[bz][:], lhsT=glr[0:17, s * 128:(s + 1) * 128], rhs=wg2b[0:17, :],
                                              start=True, stop=True),
                 reads=[t_glr, t_wg2b], writes=[t_bank[bz]])
            k.op(ACT, lambda: nc.scalar.activation(out=spb[q][:], in_=bank[bz][:], func=AF.Exp, scale=-1.0),
                 reads=[t_bank[bz]], writes=[t_sp[q], t_bank[bz]])
            k.op(ACT, lambda: nc.scalar.activation(out=spb[q][:], in_=spb[q][:], func=AF.Ln, bias=epsc[:, 1:2]),
                 reads=[t_sp[q], t_epsc], writes=[t_sp[q]])

        def gate_B(s):
            q = s % 2
            bb = 4 + s % 2
            bT = bank[bb][:].rearrange("p (h t) -> p h t", h=4)
            k.op(DVE, lambda: nc.vector.tensor_copy(out=sph[q][:], in_=spb[q][:]),
                 reads=[t_sp[q]], writes=[t_sph[q]])
            k.op(DVE, lambda: nc.vector.tensor_tensor(out=spl[q][:], in0=spb[q][:], in1=sph[q][:],
                                                      op=ALU.subtract),
                 reads=[t_sp[q], t_sph[q]], writes=[t_spl[q]])
            for h in range(4):
                k.op(PE, lambda: nc.tensor.matmul(bT[:, h, :], lhsT=sph[q][:, h * 128:(h + 1) * 128], rhs=trib[:],
                                                  start=True, stop=False),
                     reads=[t_sph[q], t_trib], writes=[t_bank[bb]], inc=False)
                k.op(PE, lambda: nc.tensor.matmul(bT[:, h, :], lhsT=spl[q][:, h * 128:(h + 1) * 128], rhs=trib[:],
                                                  start=False, stop=True),
                     reads=[t_spl[q], t_trib], writes=[t_bank[bb]], inc=(h == 3))
            k.op(ACT, lambda: nc.scalar.activation(out=Eb[:, :, s * 128:(s + 1) * 128], in_=bT, func=AF.Exp),
                 reads=[t_bank[bb]], writes=[t_Eb[s], t_bank[bb]])
            k.op(ACT, lambda: nc.scalar.activation(out=Ei[:, :, s * 128:(s + 1) * 128], in_=bT, func=AF.Exp,
                                                   scale=-1.0),
                 reads=[t_bank[bb]], writes=[t_Ei[s], t_bank[bb]])
            k.op(DVE, lambda: nc.vector.tensor_copy(out=dec[:, s, :], in_=Eb[:, :, s * 128 + 127]),
                 reads=[t_Eb[s]], writes=[t_dec[s]])

        gate_A(0)
        for s in range(NS):
            if s + 1 < NS:
                gate_A(s + 1)
            gate_B(s)

        def q_cons(c, hf, b):
            k.op(DVE, lambda: nc.vector.scalar_tensor_tensor(out=qd[:, c, hf * 512:(hf + 1) * 512], in0=bank[b][:],
                                                             scalar=128.0 ** -0.5,
                                                             in1=Eb[:, c, hf * 512:(hf + 1) * 512],
                                                             op0=ALU.mult, op1=ALU.mult),
                 reads=[t_bank[b]] + t_Eb[hf * 4:(hf + 1) * 4], writes=[t_qd[c][hf], t_bank[b]])

        def k_cons(c, hf, b):
            k.op(DVE, lambda: nc.vector.tensor_tensor(out=ki[:, c, hf * 512:(hf + 1) * 512], in0=bank[b][:],
                                                      in1=Ei[:, c, hf * 512:(hf + 1) * 512], op=ALU.mult),
                 reads=[t_bank[b]] + t_Ei[hf * 4:(hf + 1) * 4], writes=[t_ki[c][hf], t_bank[b]])

        proj_featmajor(gla_w_in_d, 0, 4, q_cons)
        proj_featmajor(gla_w_in_d, GLA_DK, 4, k_cons)
        for s in range(NS):
            i_t = 6 + cnt["t"] % 2; cnt["t"] += 1
            pt = bank_bf(i_t, "p (a b) -> p a b", a=KC)
            for h in range(4):
                k.op(PE, lambda: nc.tensor.transpose(pt[:, h, :], ki[:, h, s * 128:(s + 1) * 128], ident[:]),
                     reads=[t_ki[h][s // 4], t_ident], writes=[t_bank[i_t]], inc=(h == 3))
            k.op(ACT, lambda: nc.scalar.copy(out=kt[:, s, :].rearrange("p (h d) -> p h d", h=4), in_=pt[:, 0:4, :]),
                 reads=[t_bank[i_t]], writes=[t_kt[s], t_bank[i_t]])
        vv = carve(0, 16384, BF16, "p (s c) -> p s c", s=NS)
        t_vv = [k.view(f"vv{s}", s * 2048, (s + 1) * 2048) for s in range(NS)]
        sr = carve(16384, 16384, BF16, "p (s c) -> p s c", s=NS)
        t_sr = [k.view(f"sr{s}", 16384 + s * 2048, 16384 + (s + 1) * 2048) for s in range(NS)]

        def v_cons(p, s, b):
            k.op(ACT, lambda: nc.scalar.copy(out=vv[:, s, p * 256:(p + 1) * 256], in_=bank[b][:, 0:256]),
                 reads=[t_bank[b]], writes=[t_vv[s], t_bank[b]])

        def r_cons(p, s, b):
            q = cnt["sg"] % 2; cnt["sg"] += 1
            k.op(ACT, lambda: nc.scalar.activation(out=sg[q][:, 0:256], in_=bank[b][:, 0:256], func=AF.Silu),
                 reads=[t_bank[b]], writes=[t_sg[q], t_bank[b]])
            k.op(DVE, lambda: nc.vector.tensor_tensor(out=sr[:, s, p * 256:(p + 1) * 256], in0=sg[q][:, 0:256],
                                                      in1=ggout[:], op=ALU.mult),
                 reads=[t_sg[q], t_ggout], writes=[t_sr[s]])

        proj_tokmajor(gla_w_in_d, 2 * GLA_DK, 4, v_cons)
        proj_tokmajor(gla_w_in_d, 2 * GLA_DK + GLA_DV, 4, r_cons)
        wo, t_wo = wout_load(gla_w_out_d)

        at4 = bank[0].rearrange("p (h t) -> p h t", h=4)
        pb = [bank[4].rearrange("p (h v) -> p h v", h=2), bank[5].rearrange("p (h v) -> p h v", h=2)]

        def obank(s, h):
            bi = (2 if s % 2 == 0 else 6) + h // 2
            return bank[bi].rearrange("p (h v) -> p h v", h=2)[:, h % 2, :], t_bank[bi]

        def S1(s):
            f = s // 4
            sl = slice(s * 128, (s + 1) * 128)
            am = s % 2
            for h in range(4):
                k.op(PE, lambda: nc.tensor.matmul(at4[:, h, :], lhsT=ki[:, h, sl], rhs=qd[:, h, sl],
                                                  start=True, stop=True),
                     reads=[t_ki[h][f], t_qd[h][f]], writes=[t_bank[0]], inc=(h == 3))
            k.op(DVE, lambda: nc.vector.tensor_tensor(out=attm[am][:], in0=at4,
                                                      in1=maskT[:].unsqueeze(1).to_broadcast([128, 4, 128]),
                                                      op=ALU.mult),
                 reads=[t_bank[0], t_maskT], writes=[t_attm[am], t_bank[0]])

        def S2(s):
            f = s // 4
            sl = slice(s * 128, (s + 1) * 128)
            am = s % 2
            for h in range(4):
                k.op(PE, lambda: nc.tensor.matmul(pb[h // 2][:, h % 2, :], lhsT=kt[:, s, h * 128:(h + 1) * 128],
                                                  rhs=vv[:, s, h * 256:(h + 1) * 256],
                                                  start=True, stop=True),
                     reads=[t_kt[s], t_vv[s]], writes=[t_bank[4 + h // 2]], inc=(h % 2 == 1))
            for h in range(4):
                o_ap, t_o = obank(s, h)
                k.op(PE, lambda: nc.tensor.matmul(o_ap, lhsT=attm[am][:, h, :],
                                                  rhs=vv[:, s, h * 256:(h + 1) * 256],
                                                  start=True, stop=False),
                     reads=[t_attm[am], t_vv[s]], writes=[t_o], inc=False)
                k.op(PE, lambda: nc.tensor.matmul(o_ap, lhsT=qd[:, h, sl], rhs=Sbf[:, h, :],
                                                  start=False, stop=True),
                     reads=[t_qd[h][f], t_Sbf[h]], writes=[t_o], inc=(h % 2 == 1))
            for h in range(4):
                k.op(DVE, lambda: nc.vector.tensor_tensor(out=Sst[:, h, :], in0=pb[h // 2][:, h % 2, :],
                                                          in1=Sst[:, h, :], op=ALU.add),
                     reads=[t_bank[4 + h // 2], t_S[h]], writes=[t_S[h], t_bank[4 + h // 2]])
                k.op(DVE, lambda: nc.vector.tensor_scalar(out=Sst[:, h, :], in0=Sst[:, h, :],
                                                          scalar1=dec[:, s, h:h + 1], scalar2=None, op0=ALU.mult),
                     reads=[t_S[h], t_dec[s]], writes=[t_S[h]])
                k.op(POOL, lambda: nc.gpsimd.tensor_copy(out=Sbf[:, h, :], in_=Sst[:, h, :]),
                     reads=[t_S[h]], writes=[t_Sbf[h]])

        def S3(s):
            io = s % 2
            for h in range(4):
                o_ap, t_o = obank(s, h)
                k.op(ACT, lambda: nc.scalar.activation(out=junk[:, 0:256], in_=o_ap,
                                                       func=AF.Square, accum_out=small[:, h:h + 1]),
                     reads=[t_o], writes=[t_junk, t_small, t_o])
            k.op(ACT, lambda: nc.scalar.activation(out=small[:, 4:8], in_=small[:, 0:4], func=AF.Ln,
                                                   bias=epsc[:, 0:1], scale=1.0 / 256),
                 reads=[t_small, t_epsc], writes=[t_small])
            k.op(ACT, lambda: nc.scalar.activation(out=small[:, 8:12], in_=small[:, 4:8], func=AF.Exp, scale=-0.5),
                 reads=[t_small], writes=[t_small])
            for h in range(4):
                o_ap, t_o = obank(s, h)
                k.op(DVE, lambda: nc.vector.scalar_tensor_tensor(out=og[io][:, h * 256:(h + 1) * 256], in0=o_ap,
                                                                 scalar=small[:, 8 + h:9 + h],
                                                                 in1=sr[:, s, h * 256:(h + 1) * 256],
                                                                 op0=ALU.mult, op1=ALU.mult),
                     reads=[t_o, t_small, t_sr[s]], writes=[t_og[io], t_o])

        def S4(s):
            io = s % 2
            transpose_to_hT(lambda kc: og[io][:, kc * 128:(kc + 1) * 128], t_og[io], s, use_bank=1)

        S1(0)
        for s in range(NS):
            if s + 1 < NS:
                S1(s + 1)
            S2(s)
            if s >= 1:
                S3(s - 1)
            if s >= 2:
                S4(s - 2)
        S3(NS - 1)
        S4(NS - 2)
        S4(NS - 1)
        out_proj_fused(wo, t_wo, nxt)

    kst_sem = [k.dsem(f"kst{i}") for i in range(2)]
    vst_sem = [k.dsem(f"vst{i}") for i in range(4)]
    kc_list = [[] for _ in range(8)]
    vc_list = [[] for _ in range(8)]
    ld_sems = [k.dsem(f"kld{i}") for i in range(2)] + [k.dsem(f"vld{i}") for i in range(2)]

    def diff(n_idx, ti, nxt):
        B0 = A_BYTES + W_BYTES
        qT = carve(0, 16384, BF16, "p (h t) -> p h t", h=8)
        t_qT = [[k.view(f"qT{h}_{f}", (h * TT + f * 512) * 2, (h * TT + f * 512 + 512) * 2) for f in range(2)]
                for h in range(8)]
        kTh = [carve(16384 + i * 8192, 8192, BF16) for i in range(2)]
        t_kTh = [k.view(f"kTh{i}", 16384 + i * 8192, 16384 + (i + 1) * 8192, dsem=ld_sems[i]) for i in range(2)]
        kst = [carve(32768 + i * 2048, 2048, BF16) for i in range(2)]
        t_kst = [k.view(f"kst{i}", 32768 + i * 2048, 32768 + (i + 1) * 2048) for i in range(2)]
        vst = [carve(36864 + i * 520, 520, BF16, "p (h c) -> p h c", h=2) for i in range(4)]
        t_vst = [k.view(f"vst{i}", 36864 + i * 520, 36864 + (i + 1) * 520) for i in range(4)]
        osb = carve(39936, 4160, F32, "p (j m c) -> p j m c", j=4, m=2)
        t_osb = [k.view(f"osb{j}", 39936 + j * 1040, 39936 + (j + 1) * 1040) for j in range(4)]
        pT = [carve(B0 + i * 2048, 2048, BF16, "p (m q) -> p m q", m=2) for i in range(3)]
        t_pT = [[k.view(f"pT{i}_{j}", B0 + i * 2048 + j * 512, B0 + i * 2048 + (j + 1) * 512) for j in range(4)]
                for i in range(3)]
        VB = 8320
        Vh = [carve(B0 + 6144 + i * VB, VB, BF16, "p (b c) -> p b c", c=130) for i in range(2)]
        t_Vh = [k.view(f"Vh{i}", B0 + 6144 + i * VB, B0 + 6144 + (i + 1) * VB, dsem=ld_sems[2 + i])
                for i in range(2)]
        ogo = [B0 + 6144 + 2 * VB, A_BYTES + W_BYTES - 8192]
        og = [carve(o_, 8192, BF16, "p (j c) -> p j c", j=4) for o_ in ogo]
        t_og = [[k.view(f"dog{gi}_{j}", ogo[gi] + j * 2048, ogo[gi] + (j + 1) * 2048) for j in range(4)]
                for gi in range(2)]

        phase_norm(n_idx)
        for i in range(4):
            k.op(DVE, lambda: nc.vector.memset(vst[i][:, :, 128:130], 1.0), writes=[t_vst[i]])

        def q_cons(c, hf, b):
            k.op(ACT, lambda: nc.scalar.copy(out=qT[:, c, hf * 512:(hf + 1) * 512], in_=bank[b][:]),
                 reads=[t_bank[b]], writes=[t_qT[c][hf], t_bank[b]])

        def k_cons(c, hf, b):
            i = c % 2
            k.op(DVE, lambda: nc.vector.tensor_copy(out=kst[i][:, hf * 512:(hf + 1) * 512], in_=bank[b][:]),
                 reads=[t_bank[b]], writes=[t_kst[i], t_bank[b]])
            if hf == 1:
                tk = Tl(f"kc{c}_{ti}", kst_sem[i]); kc_list[c].append(tk)
                k.dma(SP, tk, t_kst[i], [(kcache[c, :, ti * TT:(ti + 1) * TT], kst[i][:])])

        def v_cons(p, s, b):
            i = cnt["sg"] % 4; cnt["sg"] += 1
            k.op(ACT, lambda: nc.scalar.copy(out=vst[i][:, :, 0:128],
                                             in_=bank[b][:, 0:256].rearrange("p (h c) -> p h c", h=2)),
                 reads=[t_bank[b]], writes=[t_vst[i], t_bank[b]])
            blk = ti * NS + s
            tv = Tl(f"vc{p}_{blk}", vst_sem[i])
            vc_list[2 * p].append(tv); vc_list[2 * p + 1].append(tv)
            k.dma(SP, tv, t_vst[i], [(vcache[2 * p + hh, :, blk, :], vst[i][:, hh, :]) for hh in range(2)])

        proj_featmajor(diff_w_in_d, 0, 8, q_cons)
        proj_featmajor(diff_w_in_d, D, 8, k_cons)
        proj_tokmajor(diff_w_in_d, 2 * D, 4, v_cons)
        wo, t_wo = wout_load(diff_w_out_d)

        obk = [bank[4 + j].rearrange("p (m c) -> p m c", m=2) for j in range(4)]
        units = [(g, h) for g in range(2) for h in range(8)]
        itc = [0]
        itof = {}

        def geom(g):
            nk = ti * TT + (g + 1) * 512
            return nk, nk // 128, (ti * TT + g * 512) // 128

        def loads(ui):
            g, h = units[ui]
            li = ui % 2
            nk, nkb, qb0 = geom(g)
            k.dma(SP, t_kTh[li], list(kc_list[h]), [(kTh[li][:, 0:nk], kcache[h, :, 0:nk])])
            k.dma(SP, t_Vh[li], list(vc_list[h]), [(Vh[li][:, 0:nkb, :], vcache[h, :, 0:nkb, :])])

        def qk_exp(ui, kb):
            g, h = units[ui]
            li = ui % 2
            nk, nkb, qb0 = geom(g)
            itn = itc[0]; itc[0] += 1
            itof[(ui, kb)] = itn
            j0 = max(0, kb - qb0)
            b0 = 2 * (itn % 2)
            pi = itn % 3
            c0 = j0 * 128
            for m in range(2):
                bm = b0 + m
                lhs = kTh[li][m * 64:(m + 1) * 64, kb * 128:(kb + 1) * 128]
                k.op(PE, lambda: nc.tensor.matmul(bank[bm][:, c0:512], lhsT=lhs,
                                                  rhs=qT[m * 64:(m + 1) * 64, h, g * 512 + c0:(g + 1) * 512],
                                                  start=True, stop=True),
                     reads=[t_kTh[li], t_qT[h][g]], writes=[t_bank[bm]])
            k.op(ACT, lambda: nc.scalar.activation(out=pT[pi][:, :, c0:512],
                                                   in_=psall[:, b0:b0 + 2, c0:512], func=AF.Exp,
                                                   scale=0.125),
                 reads=[t_bank[b0], t_bank[b0 + 1]], writes=t_pT[pi][j0:4] + [t_bank[b0], t_bank[b0 + 1]])
            if kb >= qb0:
                k.op(DVE, lambda: nc.vector.tensor_tensor(out=pT[pi][:, :, c0:c0 + 128],
                                                          in0=pT[pi][:, :, c0:c0 + 128],
                                                          in1=maskTb[:].unsqueeze(1).to_broadcast([128, 2, 128]),
                                                          op=ALU.mult),
                     reads=[t_pT[pi][j0], t_maskTb], writes=[t_pT[pi][j0]])

        def pv(ui, kb):
            g, h = units[ui]
            li = ui % 2
            nk, nkb, qb0 = geom(g)
            j0 = max(0, kb - qb0)
            pi = itof[(ui, kb)] % 3
            if kb == 0:
                for j in range(4):
                    k.op(PE, lambda: nc.tensor.matmul(bank[4 + j][:, :], lhsT=zer[:, 0:128], rhs=zer[:, :],
                                                      start=True, stop=False),
                         reads=[t_zer], writes=[t_bank[4 + j]], inc=False)
            for j in range(3, j0 - 1, -1):
                last = (kb == qb0 + j)
                for m in range(2):
                    k.op(PE, lambda: nc.tensor.matmul(obk[j][:, m, 0:129],
                                                      lhsT=pT[pi][:, m, j * 128:(j + 1) * 128],
                                                      rhs=Vh[li][:, kb, 0:129],
                                                      start=False, stop=(last and m == 1)),
                         reads=[t_pT[pi][j], t_Vh[li]], writes=[t_bank[4 + j]], inc=(m == 1))

        def evac():
            for j in range(4):
                tb = t_bank[4 + j]
                k.op(DVE, lambda: nc.vector.tensor_copy(out=osb[:, j, :, 0:129], in_=obk[j][:, :, 0:129]),
                     reads=[tb], writes=[t_osb[j], tb])

        def finalize_a(ui):
            for j in range(4):
                to = t_osb[j]
                k.op(DVE, lambda: nc.vector.reciprocal(out=small[:, 16 + 2 * j:18 + 2 * j],
                                                       in_=osb[:, j, :, 128]),
                     reads=[to], writes=[t_small])
                k.op(DVE, lambda: nc.vector.tensor_tensor(out=small[:, 17 + 2 * j:18 + 2 * j],
                                                          in0=small[:, 17 + 2 * j:18 + 2 * j],
                                                          in1=lams[:, 4:5], op=ALU.mult),
                     reads=[t_small, t_lams], writes=[t_small])
                k.op(DVE, lambda: nc.vector.tensor_scalar(out=osb[:, j, 1, 0:128], in0=osb[:, j, 1, 0:128],
                                                          scalar1=small[:, 17 + 2 * j:18 + 2 * j], scalar2=None,
                                                          op0=ALU.mult),
                     reads=[to, t_small], writes=[to])
                k.op(DVE, lambda: nc.vector.scalar_tensor_tensor(out=osb[:, j, 0, 0:128],
                                                                 in0=osb[:, j, 0, 0:128],
                                                                 scalar=small[:, 16 + 2 * j:17 + 2 * j],
                                                                 in1=osb[:, j, 1, 0:128],
                                                                 op0=ALU.mult, op1=ALU.add),
                     reads=[to, t_small], writes=[to])

        def finalize_b(ui):
            g, h = units[ui]
            for j in range(4):
                to = t_osb[j]
                k.op(ACT, lambda: nc.scalar.activation(out=junk[:, 0:128], in_=osb[:, j, 0, 0:128],
                                                       func=AF.Square, accum_out=small[:, 32 + j:33 + j]),
                     reads=[to], writes=[t_junk, t_small])
            k.op(ACT, lambda: nc.scalar.activation(out=small[:, 36:40], in_=small[:, 32:36], func=AF.Ln,
                                                   bias=epsc[:, 0:1], scale=1.0 / 128),
                 reads=[t_small, t_epsc], writes=[t_small])
            k.op(ACT, lambda: nc.scalar.activation(out=small[:, 40:44], in_=small[:, 36:40], func=AF.Exp,
                                                   scale=-0.5),
                 reads=[t_small], writes=[t_small])
            for j in range(4):
                k.op(DVE, lambda: nc.vector.scalar_tensor_tensor(out=og[g % 2][:, j, h * 128:(h + 1) * 128],
                                                                 in0=osb[:, j, 0, 0:128],
                                                                 scalar=small[:, 40 + j:41 + j], in1=dgout[:],
                                                                 op0=ALU.mult, op1=ALU.mult),
                     reads=[t_osb[j], t_small, t_dgout], writes=[t_og[g % 2][j]])

        def group_transposes(g):
            for j in range(4):
                transpose_to_hT(lambda kc: og[g % 2][:, j, kc * 128:(kc + 1) * 128], t_og[g % 2][j], g * 4 + j,
                                use_bank=(j if g == 0 else None))

        loads(0)
        qk_exp(0, 0)
        for ui in range(len(units)):
            g, h = units[ui]
            nk, nkb, qb0 = geom(g)
            if ui + 1 < len(units):
                loads(ui + 1)
            for kb in range(nkb):
                if kb + 1 < nkb:
                    qk_exp(ui, kb + 1)
                pv(ui, kb)
                if ui > 0 and kb == 2:
                    finalize_b(ui - 1)
                if g == 1 and h == 0 and kb == 3:
                    group_transposes(0)
            evac()
            if ui + 1 < len(units):
                qk_exp(ui + 1, 0)
            finalize_a(ui)
        finalize_b(len(units) - 1)
        group_transposes(1)
        out_proj_fused(wo, t_wo, nxt)

    def final_store(tile_i, do_norm):
        outs = []
        if do_norm:
            gf = carve(0, 4096, F32); t_gf = k.view("gfin", 0, 4096, dsem=gf_sem)
            ob = [carve(4096 + i * 4096, 4096, F32) for i in range(2)]
            k.dma(SP, t_gf, None, [(gf[:], gfin_d)])
            if pre_done["v"]:
                pre_done["v"] = False
            else:
                rms_stats()
        for s in range(NS):
            r0 = tile_i * TT + s * 128
            if do_norm:
                i = s % 2
                t_ob = k.view(f"ob{i}", 4096 + i * 4096, 8192 + i * 4096, dsem=None)
                k.op(DVE, lambda: nc.vector.scalar_tensor_tensor(out=ob[i][:], in0=xt[:, s, :],
                                                                 scalar=rstd[:, s:s + 1], in1=gf[:],
                                                                 op0=ALU.mult, op1=ALU.mult),
                     reads=[t_xt[s], t_rstd[s], t_gf], writes=[t_ob])
                t_out = Tl(f"o{tile_i}_{s}", out_sems[i])
                k.dma(SP, t_out, t_ob, [(out_d[r0:r0 + 128, :], ob[i][:])])
            else:
                t_out = Tl(f"o{tile_i}_{s}", out_sems[s % 2])
                k.dma(SP, t_out, t_xt[s], [(out_d[r0:r0 + 128, :], xt[:, s, :])])
            outs.append(t_out)
        return outs

    for ti in range(ntiles):
        if ti == 0 or stages < 7:
            load_x(ti)
        if stages >= 1:
            ffn(0, 0, nxt=1 if stages >= 2 else None)
        if stages >= 2:
            gla(1, 2)
        if stages >= 3:
            ffn(1, 2, nxt=3 if stages >= 4 else None)
        if stages >= 4:
            ffn(2, 3, nxt=4 if stages >= 5 else None)
        if stages >= 5:
            diff(4, ti, 5)
        if stages >= 6:
            ffn(3, 5, nxt="final" if stages >= 7 else None, ti=ti, has_next=(ti + 1 < ntiles))
        if stages < 7:
            pre_done["v"] = False
            last_out += final_store(ti, False)
    for t in last_out[-4:]:
        k.wait_tile(SP, t)
    for key in out_sems:
        SP.h.wait_ge(k.semof[key], k.dcnt[key])
    print(f"[build] inst={k.ninst} waits={k.nwaits} sems={k.nsem}")
    stack.close()
    return nc


def _consts():
    c = {}
    c["ident"] = np.eye(128, dtype=np.float32).astype(ml_dtypes.bfloat16)
    s = np.arange(128)[:, None]
    t = np.arange(128)[None, :]
    c["tri"] = np.where(s <= t, -1.0 / 16.0, 0.0).astype(np.float32)
    c["maskT"] = np.where(s <= t, 1.0, 0.0).astype(np.float32)
    c["negmask"] = np.where(s > t, NEG, 0.0).astype(np.float32).astype(ml_dtypes.bfloat16)
    return c


def make_in_maps(inputs, n_cores=8):
    f = lambda n: np.ascontiguousarray(np.asarray(inputs[n], np.float32))
    x = f("x")
    shared = dict(_consts())
    norm_names = ["l0_norm_ffn1", "l0_norm_mix", "l0_norm_ffn2", "l1_norm_ffn1", "l1_norm_mix", "l1_norm_ffn2",
                  "final_norm"]
    norms = np.stack([f(n).reshape(KC, 128).T for n in norm_names], axis=1)
    shared["norms"] = np.ascontiguousarray(norms)
    ffn_names = [("l0_ffn1_w_in", "l0_ffn1_w_down"), ("l0_ffn2_w_in", "l0_ffn2_w_down"),
                 ("l1_ffn1_w_in", "l1_ffn1_w_down"), ("l1_ffn2_w_in", "l1_ffn2_w_down")]
    for i, (a, b) in enumerate(ffn_names):
        shared[f"ffn{i}_w_in"] = f(a)
        shared[f"ffn{i}_w_down"] = f(b)
    shared["gla_w_in"] = f("l0_gla_w_in")
    shared["gla_wg2b"] = np.ascontiguousarray(np.concatenate([f("l0_gla_w_gate2"), f("l0_gla_b_gate2")[None, :]], 0))
    shared["gla_gout"] = np.ascontiguousarray(np.broadcast_to(f("l0_gla_norm_out")[None, :], (128, 256)))
    shared["gla_w_out"] = f("l0_gla_w_out")
    shared["diff_w_in"] = f("l1_diff_w_in")
    lam = np.stack([f("l1_diff_lambda_q1"), f("l1_diff_lambda_k1"), f("l1_diff_lambda_q2"), f("l1_diff_lambda_k2")])
    shared["diff_lam"] = np.ascontiguousarray(np.broadcast_to(lam[None], (128, 4, 64)))
    shared["diff_gout"] = np.ascontiguousarray(np.broadcast_to(f("l1_diff_norm_out")[None, :], (128, 128)))
    shared["diff_w_out"] = f("l1_diff_w_out")
    shared["gfin"] = np.ascontiguousarray(np.broadcast_to(f("final_norm")[None, :], (128, D)))
    maps = []
    for ci in range(n_cores):
        m = dict(shared)
        m["x"] = np.ascontiguousarray(x[ci])
        maps.append(m)
    return maps


def kernel(**inputs):
    nc = build_nc()
    in_maps = make_in_maps(inputs, 8)
    res = run_bass_kernel_spmd(nc, in_maps, core_ids=list(range(8)))
    return np.stack([np.asarray(r["out"], dtype=np.float32) for r in res.results], axis=0)
```

```python
import numpy as np
import ml_dtypes
from contextlib import ExitStack
import concourse.bass as bass
import concourse.mybir as mybir
from concourse.bass_utils import run_bass_kernel_spmd

F32 = mybir.dt.float32
BF16 = mybir.dt.bfloat16
ALU = mybir.AluOpType
AF = mybir.ActivationFunctionType

D = 1024
SEQ = 4096
DFF = 2816
NHC = DFF // 128
KC = D // 128
TT = 1024
NS = TT // 128
NT = SEQ // TT
EPS = 1e-6
GLA_DK, GLA_DV, GLA_R = 512, 1024, 16
GLA_IN = 2 * GLA_DK + 2 * GLA_DV + GLA_R
LAMBDA_INIT = 0.8 - 0.6 * float(np.exp(-0.3))
SEM_EPOCH = 30000


class Eng:
    def __init__(self, k, name, h, ordered=False):
        self.k, self.name, self.h, self.ordered = k, name, h, ordered
        self.epoch = -1
        self.count = 0
        self.known = {}
        self.hist = {}
        self.new_epoch()

    def new_epoch(self):
        self.epoch += 1
        self.sem = self.k.new_sem(f"{self.name}{self.epoch}")
        self.key = (self.name, self.epoch)
        self.k.semof[self.key] = self.sem
        self.count = 0
        self.pending = False


class Tl:
    def __init__(self, name, dsem=None):
        self.name = name
        self.w = None
        self.r = []
        self.dsem = dsem
        self.dcount = 0


class K:
    def __init__(self, nc, stack):
        self.nc, self.stack = nc, stack
        self.semof = {}
        self.nsem = 0
        self.pe = Eng(self, "pe", nc.tensor, ordered=True)
        self.act = Eng(self, "act", nc.scalar)
        self.dve = Eng(self, "dve", nc.vector)
        self.pool = Eng(self, "pool", nc.gpsimd)
        self.sp = Eng(self, "sp", nc.sync)
        self.nwaits = 0
        self.ninst = 0
        self.dcnt = {}
        self.live = []

    def new_sem(self, name):
        self.nsem += 1
        return self.stack.enter_context(self.nc.semaphore(f"s_{name}_{self.nsem}"))

    def dsem(self, name):
        key = ("dma", name, self.nsem)
        self.semof[key] = self.new_sem("d" + name)
        self.dcnt[key] = 0
        return key

    def tile(self, name, dma=False):
        t = Tl(name)
        if dma:
            t.dsem = self.dsem(name)
        return t

    def view(self, name, lo, hi, dsem=None):
        t = Tl(name, dsem)
        t.lo, t.hi = lo, hi
        pend = {}
        for o in list(self.live):
            if o.lo < hi and lo < o.hi:
                for d in ([o.w] if o.w else []) + o.r:
                    if pend.get(d[0], 0) < d[1]:
                        pend[d[0]] = d[1]
                self.live.remove(o)
                o.dead = True
        t.r = list(pend.items())
        self.live.append(t)
        return t

    def _deps(self, eng, reads, writes):
        deps = {}
        for t in list(reads) + list(writes):
            if getattr(t, "dead", False):
                raise RuntimeError(f"use of retired view {t.name}")
        def add(d):
            if d is None:
                return
            key, c = d
            if deps.get(key, 0) < c:
                deps[key] = c
        for t in reads:
            add(t.w)
        for t in writes:
            add(t.w)
            for d in t.r:
                add(d)
        return deps

    def _wait(self, eng, deps):
        for key, c in deps.items():
            if eng.ordered and key[0] == eng.name:
                continue
            if eng.known.get(key, 0) >= c:
                continue
            if key[0] != "dma":
                src = getattr(self, key[0])
                if key == src.key and c > src.count:
                    raise RuntimeError(f"wait on unissued inc: {eng.name} waits {key} {c} > {src.count}")
            eng.h.wait_ge(self.semof[key], c)
            self.nwaits += 1
            eng.known[key] = c
            hk = self.hist_lookup(key, c)
            if hk:
                for k2, c2 in hk.items():
                    if eng.known.get(k2, 0) < c2:
                        eng.known[k2] = c2

    def hist_lookup(self, key, c):
        if key[0] == "dma":
            return None
        src = getattr(self, key[0])
        return src.hist.get((key, c))

    def op(self, eng, fn, reads=(), writes=(), inc=True):
        deps = self._deps(eng, reads, writes)
        self._wait(eng, deps)
        ins = fn()
        self.ninst += 1
        if inc:
            if eng.count >= SEM_EPOCH and not eng.pending:
                eng.new_epoch()
            eng.count += 1
            ins.then_inc(eng.sem, 1)
            my = (eng.key, eng.count)
            eng.hist[my] = dict(eng.known)
            eng.pending = False
        else:
            my = (eng.key, eng.count + 1)
            eng.pending = True
        for t in reads:
            t.r.append(my)
        for t in writes:
            t.w = my
            t.r = []
        return ins

    def dma(self, q, dst, src, pairs):
        assert dst.dsem is not None
        srcs = [] if src is None else (list(src) if isinstance(src, (list, tuple)) else [src])
        deps = self._deps(q, srcs, [dst])
        self._wait(q, deps)
        sem = self.semof[dst.dsem]
        for o, i in pairs:
            q.h.dma_start(out=o, in_=i).then_inc(sem, 16)
            self.dcnt[dst.dsem] += 16
        my = (dst.dsem, self.dcnt[dst.dsem])
        for t in srcs:
            t.r.append(my)
        dst.w = my
        dst.r = []

    def wait_tile(self, eng, t):
        self._wait(eng, self._deps(eng, [t], []))


A_BYTES = NHC * TT * 2
W_BYTES = NHC * D * 2
B_BYTES = 32768
WORK_BYTES = A_BYTES + W_BYTES + B_BYTES
NEG = -30000.0


def build_nc(stages=99, ntiles=NT):
    nc = bass.Bass("TRN2", target_bir_lowering=False)
    stack = ExitStack()
    k = K(nc, stack)
    PE, ACT, DVE, POOL, SP = k.pe, k.act, k.dve, k.pool, k.sp

    def din(name, shape, dt=F32):
        return nc.dram_tensor(name, list(shape), dt, kind="ExternalInput").ap()

    x_d = din("x", [SEQ, D])
    out_d = nc.dram_tensor("out", [SEQ, D], F32, kind="ExternalOutput").ap()
    ident_d = din("ident", [128, 128], BF16)
    norms_d = din("norms", [128, 7, KC])
    ffn_w_in = [din(f"ffn{i}_w_in", [D, 2 * DFF]) for i in range(4)]
    ffn_w_down = [din(f"ffn{i}_w_down", [DFF, D]) for i in range(4)]
    gla_w_in_d = din("gla_w_in", [D, GLA_IN])
    gla_wg2b_d = din("gla_wg2b", [17, GLA_DK])
    gla_gout_d = din("gla_gout", [128, 256])
    gla_w_out_d = din("gla_w_out", [D, D])
    diff_w_in_d = din("diff_w_in", [D, 3 * D])
    diff_lam_d = din("diff_lam", [128, 4, 64])
    diff_gout_d = din("diff_gout", [128, 128])
    diff_w_out_d = din("diff_w_out", [D, D])
    tri_d = din("tri", [128, 128])
    maskT_d = din("maskT", [128, 128])
    negmask_d = din("negmask", [128, 128], BF16)
    gfin_d = din("gfin", [128, D])
    kcache = nc.dram_tensor("kcache", [8, 128, SEQ], BF16).ap()
    vcache = nc.dram_tensor("vcache", [8, 128, SEQ // 128, 130], BF16).ap()

    def sb(name, shape, dt):
        return stack.enter_context(nc.sbuf_tensor("sb_" + name, list(shape), dt))

    ident = sb("ident", [128, 128], BF16); t_ident = k.tile("ident", dma=True)
    norms = sb("norms", [128, 7, KC], F32); t_norms = k.tile("norms", dma=True)
    xt = sb("xt", [128, NS, D], F32)
    t_xt = [k.tile(f"xt{s}", dma=True) for s in range(NS)]
    hT = sb("hT", [128, KC, TT], BF16)
    t_hT = [k.tile(f"hT{s}") for s in range(NS)]
    NRING = 3
    wring = [sb(f"wring{i}", [128, KC, 256], BF16) for i in range(NRING)]
    t_wring = [k.tile(f"wring{i}", dma=True) for i in range(NRING)]
    ring_pos = [0]
    hb = [sb(f"hb{i}", [128, D], BF16) for i in range(2)]
    t_hb = [k.tile(f"hb{i}") for i in range(2)]
    junk = sb("junk", [128, D], BF16); t_junk = k.tile("junk")
    sg = [sb(f"sg{i}", [128, 512], F32) for i in range(2)]
    t_sg = [k.tile(f"sg{i}") for i in range(2)]
    ssq = sb("ssq", [128, 3 * NS], F32); t_ssq = [k.tile(f"ssq{s}") for s in range(NS)]
    rstd = sb("rstd", [128, NS], F32); t_rstd = [k.tile(f"rstd{s}") for s in range(NS)]
    epsc = sb("epsc", [128, 2], F32); t_epsc = k.tile("epsc")
    tri = sb("tri", [128, 128], F32); t_tri = k.tile("tri", dma=True)
    maskT = sb("maskT", [128, 128], F32); t_maskT = k.tile("maskT", dma=True)
    negmask = sb("negmask", [128, 128], BF16); t_negmask = k.tile("negmask", dma=True)
    maskTb = sb("maskTb", [128, 128], BF16); t_maskTb = k.tile("maskTb")
    zer = sb("zer", [128, 512], BF16); t_zer = k.tile("zer")
    wg2b = sb("wg2b", [17, GLA_DK], BF16); t_wg2b = k.tile("wg2b", dma=True)
    trib = sb("trib", [128, 128], BF16); t_trib = k.tile("trib")
    ggout = sb("ggout", [128, 256], F32); t_ggout = k.tile("ggout", dma=True)
    dgout = sb("dgout", [128, 128], F32); t_dgout = k.tile("dgout", dma=True)
    lamv = sb("lamv", [128, 4, 64], F32); t_lamv = k.tile("lamv", dma=True)
    lams = sb("lams", [128, 8], F32); t_lams = k.tile("lams")
    Sst = sb("Sst", [128, 4, 256], F32); t_S = [k.tile(f"S{h}") for h in range(4)]
    Sbf = sb("Sbf", [128, 4, 256], BF16); t_Sbf = [k.tile(f"Sbf{h}") for h in range(4)]
    small = sb("small", [128, 64], F32); t_small = k.tile("small")
    rem = nc.sbuf_bytes_remaining
    print(f"[build] sbuf remaining before work: {rem}, work={WORK_BYTES}")
    work = sb("work", [128, WORK_BYTES // 2], BF16)

    def carve(lo, nbytes, dt, pattern=None, **kw):
        ap = work[:, lo // 2:(lo + nbytes) // 2]
        if dt == F32:
            ap = ap.bitcast(F32)
        if pattern:
            ap = ap.rearrange(pattern, **kw)
        return ap

    psall = stack.enter_context(nc.psum_tensor("ps_all", [128, 8, 512], F32))
    bank = [psall[:, i, :] for i in range(8)]
    t_bank = [k.tile(f"bank{i}") for i in range(8)]

    def bank_bf(i, pattern=None, **kw):
        ap = bank[i].bitcast(BF16)
        if pattern:
            ap = ap.rearrange(pattern, **kw)
        return ap

    cnt = {"gu": 0, "o": 0, "t": 0, "hb": 0, "sg": 0, "fm": 0, "tm": 0}

    k.op(DVE, lambda: nc.vector.memset(epsc[:, 0:1], EPS), writes=[t_epsc])
    k.op(DVE, lambda: nc.vector.memset(epsc[:, 1:2], 1.0), writes=[t_epsc])
    k.dma(SP, t_ident, None, [(ident[:], ident_d)])
    k.dma(SP, t_norms, None, [(norms[:], norms_d)])
    k.dma(SP, t_tri, None, [(tri[:], tri_d)])
    k.dma(SP, t_maskT, None, [(maskT[:], maskT_d)])
    k.dma(SP, t_negmask, None, [(negmask[:], negmask_d)])
    k.op(DVE, lambda: nc.vector.tensor_copy(out=maskTb[:], in_=maskT[:]), reads=[t_maskT], writes=[t_maskTb])
    k.op(DVE, lambda: nc.vector.memset(zer[:], 0.0), writes=[t_zer])
    k.dma(POOL, t_wg2b, None, [(wg2b[:], gla_wg2b_d)])
    k.op(DVE, lambda: nc.vector.tensor_copy(out=trib[:], in_=tri[:]), reads=[t_tri], writes=[t_trib])
    k.dma(SP, t_ggout, None, [(ggout[:], gla_gout_d)])
    k.dma(SP, t_dgout, None, [(dgout[:], diff_gout_d)])
    k.dma(SP, t_lamv, None, [(lamv[:], diff_lam_d)])
    for h in range(4):
        k.op(DVE, lambda: nc.vector.memset(Sst[:, h, :], 0.0), writes=[t_S[h]])
        k.op(DVE, lambda: nc.vector.memset(Sbf[:, h, :], 0.0), writes=[t_Sbf[h]])
    k.op(DVE, lambda: nc.vector.tensor_scalar(out=dgout[:], in0=dgout[:], scalar1=1.0 - LAMBDA_INIT, scalar2=None,
                                              op0=ALU.mult), reads=[t_dgout], writes=[t_dgout])
    k.op(DVE, lambda: nc.vector.tensor_tensor(out=lamv[:, 0, :], in0=lamv[:, 0, :], in1=lamv[:, 1, :], op=ALU.mult),
         reads=[t_lamv], writes=[t_lamv])
    k.op(DVE, lambda: nc.vector.tensor_tensor(out=lamv[:, 2, :], in0=lamv[:, 2, :], in1=lamv[:, 3, :], op=ALU.mult),
         reads=[t_lamv], writes=[t_lamv])
    k.op(DVE, lambda: nc.vector.reduce_sum(out=lams[:, 0:1], in_=lamv[:, 0, :], axis=mybir.AxisListType.X),
         reads=[t_lamv], writes=[t_lams])
    k.op(DVE, lambda: nc.vector.reduce_sum(out=lams[:, 1:2], in_=lamv[:, 2, :], axis=mybir.AxisListType.X),
         reads=[t_lamv], writes=[t_lams])
    k.op(ACT, lambda: nc.scalar.activation(out=lams[:, 2:4], in_=lams[:, 0:2], func=AF.Exp),
         reads=[t_lams], writes=[t_lams])
    k.op(DVE, lambda: nc.vector.scalar_tensor_tensor(out=lams[:, 4:5], in0=lams[:, 3:4], scalar=-LAMBDA_INIT,
                                                     in1=lams[:, 2:3], op0=ALU.add, op1=ALU.subtract),
         reads=[t_lams], writes=[t_lams])

    def load_x(tile_i):
        for s in range(NS):
            r0 = tile_i * TT + s * 128
            k.dma(SP, t_xt[s], None, [(xt[:, s, :], x_d[r0:r0 + 128, :])])

    def norm_sub_stats(s):
        k.op(ACT, lambda: nc.scalar.activation(out=junk[:], in_=xt[:, s, :], func=AF.Square,
                                               accum_out=ssq[:, s:s + 1]),
             reads=[t_xt[s]], writes=[t_junk, t_ssq[s]])
        k.op(ACT, lambda: nc.scalar.activation(out=ssq[:, NS + s:NS + s + 1], in_=ssq[:, s:s + 1], func=AF.Ln,
                                               bias=epsc[:, 0:1], scale=1.0 / D),
             reads=[t_ssq[s], t_epsc], writes=[t_ssq[s]])
        k.op(ACT, lambda: nc.scalar.activation(out=rstd[:, s:s + 1], in_=ssq[:, NS + s:NS + s + 1], func=AF.Exp,
                                               scale=-0.5),
             reads=[t_ssq[s]], writes=[t_rstd[s]])

    def rms_stats():
        for s in range(NS):
            norm_sub_stats(s)

    def norm_sub_hb(s, on_act=False):
        i_hb = cnt["hb"] % 2; cnt["hb"] += 1
        if on_act:
            k.op(ACT, lambda: nc.scalar.activation(out=hb[i_hb][:], in_=xt[:, s, :], func=AF.Copy,
                                                   scale=rstd[:, s:s + 1]),
                 reads=[t_rstd[s], t_xt[s]], writes=[t_hb[i_hb]])
        else:
            k.op(DVE, lambda: nc.vector.tensor_scalar(out=hb[i_hb][:], in0=xt[:, s, :], scalar1=rstd[:, s:s + 1],
                                                      scalar2=None, op0=ALU.mult),
                 reads=[t_rstd[s], t_xt[s]], writes=[t_hb[i_hb]])
        return i_hb

    def norm_sub_tr(s, i_hb, n_idx):
        transpose_to_hT(lambda kc: hb[i_hb][:, kc * 128:(kc + 1) * 128], t_hb[i_hb], s, gscale_idx=n_idx)

    def transpose_to_hT(src_ap_fn, src_tile, s, gscale_idx=None, use_bank=None):
        if use_bank is None:
            i_t = 6 + cnt["t"] % 2; cnt["t"] += 1
        else:
            i_t = use_bank
        pt = bank_bf(i_t, "p (a b) -> p a b", a=KC)
        for kc in range(KC):
            k.op(PE, lambda: nc.tensor.transpose(pt[:, kc, :], src_ap_fn(kc), ident[:]),
                 reads=[src_tile, t_ident], writes=[t_bank[i_t]], inc=(kc == KC - 1))
        if gscale_idx is None:
            k.op(ACT, lambda: nc.scalar.copy(out=hT[:, :, s * 128:(s + 1) * 128], in_=pt[:, :, :]),
                 reads=[t_bank[i_t]], writes=[t_hT[s], t_bank[i_t]])
        else:
            k.op(DVE, lambda: nc.vector.tensor_tensor(out=hT[:, :, s * 128:(s + 1) * 128], in0=pt[:, :, :],
                                                      in1=norms[:, gscale_idx, :].unsqueeze(2).to_broadcast(
                                                          [128, KC, 128]),
                                                      op=ALU.mult),
                 reads=[t_bank[i_t], t_norms], writes=[t_hT[s], t_bank[i_t]])

    def norm_to_hT(n_idx):
        rms_stats()
        for s in range(NS):
            i_hb = norm_sub_hb(s)
            norm_sub_tr(s, i_hb, n_idx)

    pre_done = {"v": False}

    def phase_norm(n_idx):
        if pre_done["v"]:
            pre_done["v"] = False
        else:
            norm_to_hT(n_idx)

    def ring_load(w_d, col_specs):
        i = ring_pos[0] % NRING; ring_pos[0] += 1
        pairs = []
        for (d0, s0, n) in col_specs:
            pairs.append((wring[i][:, :, d0:d0 + n],
                          w_d[:, s0:s0 + n].rearrange("(kc p) c -> p kc c", p=128)))
        k.dma(POOL, t_wring[i], None, pairs)
        return i

    def proj_featmajor(w_d, col0, nchunks, consumer, halves=(0, 1), banks=(0, 1, 2, 3)):
        for c0 in range(0, nchunks, 2):
            ncol = min(2, nchunks - c0) * 128
            i = ring_load(w_d, [(0, col0 + c0 * 128, ncol)])
            for cc in range(ncol // 128):
                for hf in halves:
                    b = banks[cnt["fm"] % len(banks)]; cnt["fm"] += 1
                    for kc in range(KC):
                        k.op(PE, lambda: nc.tensor.matmul(bank[b][:], lhsT=wring[i][:, kc, cc * 128:(cc + 1) * 128],
                                                          rhs=hT[:, kc, hf * 512:(hf + 1) * 512],
                                                          start=(kc == 0), stop=(kc == KC - 1)),
                             reads=[t_wring[i]] + t_hT[hf * 4:(hf + 1) * 4], writes=[t_bank[b]],
                             inc=(kc == KC - 1))
                    consumer(c0 + cc, hf, b)

    def proj_tokmajor(w_d, col0, npanels, consumer, banks=(4, 5)):
        for p in range(npanels):
            i = ring_load(w_d, [(0, col0 + p * 256, 256)])
            for s in range(NS):
                b = banks[cnt["tm"] % len(banks)]; cnt["tm"] += 1
                for kc in range(KC):
                    k.op(PE, lambda: nc.tensor.matmul(bank[b][:, 0:256], lhsT=hT[:, kc, s * 128:(s + 1) * 128],
                                                      rhs=wring[i][:, kc, :],
                                                      start=(kc == 0), stop=(kc == KC - 1)),
                         reads=[t_wring[i], t_hT[s]], writes=[t_bank[b]], inc=(kc == KC - 1))
                consumer(p, s, b)

    def resid_add_consumer(p, s, b):
        k.op(DVE, lambda: nc.vector.tensor_tensor(out=xt[:, s, p * 256:(p + 1) * 256], in0=bank[b][:, 0:256],
                                                  in1=xt[:, s, p * 256:(p + 1) * 256], op=ALU.add),
             reads=[t_bank[b], t_xt[s]], writes=[t_xt[s], t_bank[b]])

    wdn_sems = [k.dsem(f"wdn{c}") for c in range(NHC)]
    out_sems = [k.dsem(f"out{i}") for i in range(2)]
    gf_sem = k.dsem("gfin")
    last_out = []

    def ffn(fi, n_idx, nxt=None, ti=None, has_next=False):
        w_in, w_dn = ffn_w_in[fi], ffn_w_down[fi]
        act = carve(0, A_BYTES, BF16, "p (c t) -> p c t", c=NHC)
        t_act = [[k.view(f"act{c}_{h}", (c * TT + h * 512) * 2, (c * TT + h * 512 + 512) * 2) for h in range(2)]
                 for c in range(NHC)]
        wdn = carve(A_BYTES, W_BYTES, BF16, "p (c t) -> p c t", c=NHC)
        t_wdn = [k.view(f"wdn{c}", A_BYTES + c * D * 2, A_BYTES + (c + 1) * D * 2, dsem=wdn_sems[c])
                 for c in range(NHC)]
        phase_norm(n_idx)
        PRE = 2
        slots = {}
        for c in range(min(PRE, NHC)):
            slots[c] = ring_load(w_in, [(0, c * 128, 128), (128, DFF + c * 128, 128)])
        for c in range(NHC):
            if c + PRE < NHC:
                slots[c + PRE] = ring_load(w_in, [(0, (c + PRE) * 128, 128), (128, DFF + (c + PRE) * 128, 128)])
            k.dma(POOL, t_wdn[c], None, [(wdn[:, c, :], w_dn[c * 128:(c + 1) * 128, :])])
            i = slots[c]
            for hf in range(2):
                j = cnt["gu"] % 2; cnt["gu"] += 1
                bg, bu = j, 2 + j
                hts = t_hT[hf * 4:(hf + 1) * 4]
                for kc in range(KC):
                    k.op(PE, lambda: nc.tensor.matmul(bank[bg][:], lhsT=wring[i][:, kc, 0:128],
                                                      rhs=hT[:, kc, hf * 512:(hf + 1) * 512],
                                                      start=(kc == 0), stop=(kc == KC - 1)),
                         reads=[t_wring[i]] + hts, writes=[t_bank[bg]], inc=(kc == KC - 1))
                for kc in range(KC):
                    k.op(PE, lambda: nc.tensor.matmul(bank[bu][:], lhsT=wring[i][:, kc, 128:256],
                                                      rhs=hT[:, kc, hf * 512:(hf + 1) * 512],
                                                      start=(kc == 0), stop=(kc == KC - 1)),
                         reads=[t_wring[i]] + hts, writes=[t_bank[bu]], inc=(kc == KC - 1))
                q = cnt["sg"] % 2; cnt["sg"] += 1
                k.op(ACT, lambda: nc.scalar.activation(out=sg[q][:], in_=bank[bg][:], func=AF.Silu),
                     reads=[t_bank[bg]], writes=[t_sg[q], t_bank[bg]])
                k.op(DVE, lambda: nc.vector.tensor_tensor(out=act[:, c, hf * 512:(hf + 1) * 512],
                                                          in0=bank[bu][:], in1=sg[q][:], op=ALU.mult),
                     reads=[t_sg[q], t_bank[bu]], writes=[t_act[c][hf], t_bank[bu]])
        pend = None
        fin = {}
        if nxt == "final":
            B0f = A_BYTES + W_BYTES
            gf = carve(B0f, 4096, F32); t_gf = k.view("gfin", B0f, B0f + 4096, dsem=gf_sem)
            obf = [carve(B0f + 4096 + i * 4096, 4096, F32) for i in range(2)]
            k.dma(SP, t_gf, None, [(gf[:], gfin_d)])

        def fin_step(step):
            a, b, c = step - 1, step - 2, step - 3
            if 0 <= a < NS:
                norm_sub_stats(a)
                i = a % 2
                t_ob = k.view(f"ob{i}", B0f + 4096 + i * 4096, B0f + 8192 + i * 4096, dsem=None)
                k.op(DVE, lambda: nc.vector.scalar_tensor_tensor(out=obf[i][:], in0=xt[:, a, :],
                                                                 scalar=rstd[:, a:a + 1], in1=gf[:],
                                                                 op0=ALU.mult, op1=ALU.mult),
                     reads=[t_xt[a], t_rstd[a], t_gf], writes=[t_ob])
                r0 = ti * TT + a * 128
                t_out = Tl(f"o{ti}_{a}", out_sems[i])
                k.dma(SP, t_out, t_ob, [(out_d[r0:r0 + 128, :], obf[i][:])])
                last_out.append(t_out)
                if has_next:
                    r1 = (ti + 1) * TT + a * 128
                    k.dma(SP, t_xt[a], None, [(xt[:, a, :], x_d[r1:r1 + 128, :])])
            if has_next and 0 <= b < NS:
                norm_sub_stats(b)
                fin[b] = norm_sub_hb(b)
            if has_next and 0 <= c < NS:
                norm_sub_tr(c, fin[c], 0)

        for s in range(NS):
            if nxt == "final":
                fin_step(s)
            elif s > 0 and nxt is not None:
                norm_sub_stats(s - 1)
                if pend is not None:
                    norm_sub_tr(*pend)
                pend = (s - 1, norm_sub_hb(s - 1), nxt)
            for of in range(2):
                j = 4 + cnt["o"] % 2; cnt["o"] += 1
                for c in range(NHC):
                    k.op(PE, lambda: nc.tensor.matmul(bank[j][:], lhsT=act[:, c, s * 128:(s + 1) * 128],
                                                      rhs=wdn[:, c, of * 512:(of + 1) * 512],
                                                      start=(c == 0), stop=(c == NHC - 1)),
                         reads=[t_act[c][s // 4], t_wdn[c]], writes=[t_bank[j]], inc=(c == NHC - 1))
                k.op(DVE, lambda: nc.vector.scalar_tensor_tensor(out=xt[:, s, of * 512:(of + 1) * 512],
                                                                 in0=bank[j][:], scalar=0.5,
                                                                 in1=xt[:, s, of * 512:(of + 1) * 512],
                                                                 op0=ALU.mult, op1=ALU.add),
                     reads=[t_bank[j], t_xt[s]], writes=[t_xt[s], t_bank[j]])

        if nxt == "final":
            for st in range(NS, NS + 3):
                fin_step(st)
            pre_done["v"] = has_next
        elif nxt is not None:
            norm_sub_stats(NS - 1)
            if pend is not None:
                norm_sub_tr(*pend)
            norm_sub_tr(NS - 1, norm_sub_hb(NS - 1), nxt)
            pre_done["v"] = True

    wo_sems = [k.dsem(f"wo{p}") for p in range(4)]

    def wout_load(w_d):
        wo = [carve(A_BYTES + p * 4096, 4096, BF16, "p (kc c) -> p kc c", kc=KC) for p in range(4)]
        t_wo = [k.view(f"wo{p}", A_BYTES + p * 4096, A_BYTES + (p + 1) * 4096, dsem=wo_sems[p]) for p in range(4)]
        for p in range(4):
            k.dma(POOL, t_wo[p], None,
                  [(wo[p][:, :, :], w_d[:, p * 256:(p + 1) * 256].rearrange("(kc p) c -> p kc c", p=128))])
        return wo, t_wo

    def out_proj_fused(wo, t_wo, nxt):
        pend = None
        for s in range(NS):
            for p in range(4):
                b = 4 + cnt["tm"] % 2; cnt["tm"] += 1
                for kc in range(KC):
                    k.op(PE, lambda: nc.tensor.matmul(bank[b][:, 0:256], lhsT=hT[:, kc, s * 128:(s + 1) * 128],
                                                      rhs=wo[p][:, kc, :], start=(kc == 0), stop=(kc == KC - 1)),
                         reads=[t_wo[p], t_hT[s]], writes=[t_bank[b]], inc=(kc == KC - 1))
                resid_add_consumer(p, s, b)
            norm_sub_stats(s)
            if pend is not None:
                norm_sub_tr(*pend)
            pend = (s, norm_sub_hb(s, on_act=True), nxt)
        norm_sub_tr(*pend)
        pre_done["v"] = True

    def gla(n_idx, nxt):
        B0 = A_BYTES + W_BYTES
        Eb = carve(0, 16384, F32, "p (h t) -> p h t", h=4)
        t_Eb = [k.view(f"Eb{s}", s * 2048, (s + 1) * 2048) for s in range(NS)]
        Ei = carve(16384, 16384, F32, "p (h t) -> p h t", h=4)
        t_Ei = [k.view(f"Ei{s}", 16384 + s * 2048, 16384 + (s + 1) * 2048) for s in range(NS)]
        qd = carve(32768, 8192, BF16, "p (h t) -> p h t", h=4)
        t_qd = [[k.view(f"qd{h}_{f}", 32768 + (h * TT + f * 512) * 2, 32768 + (h * TT + f * 512 + 512) * 2)
                 for f in range(2)] for h in range(4)]
        glr = carve(40960, 2048, BF16); t_glr = k.view("glr", 40960, 43008)
        sph = [carve(43008 + i * 1024, 1024, BF16) for i in range(2)]
        t_sph = [k.view(f"sph{i}", 43008 + i * 1024, 43008 + (i + 1) * 1024) for i in range(2)]
        ki = carve(B0, 8192, BF16, "p (h t) -> p h t", h=4)
        t_ki = [[k.view(f"ki{h}_{f}", B0 + (h * TT + f * 512) * 2, B0 + (h * TT + f * 512 + 512) * 2)
                 for f in range(2)] for h in range(4)]
        kt = carve(B0 + 8192, 8192, BF16, "p (s d) -> p s d", s=NS)
        t_kt = [k.view(f"kt{s}", B0 + 8192 + s * 1024, B0 + 8192 + (s + 1) * 1024) for s in range(NS)]
        spb = [carve(B0 + 16384 + i * 2048, 2048, F32) for i in range(2)]
        t_sp = [k.view(f"sp{i}", B0 + 16384 + i * 2048, B0 + 16384 + (i + 1) * 2048) for i in range(2)]
        og = [carve(B0 + 20480 + i * 2048, 2048, BF16) for i in range(2)]
        t_og = [k.view(f"og{i}", B0 + 20480 + i * 2048, B0 + 20480 + (i + 1) * 2048) for i in range(2)]
        attm = [carve(B0 + 24576 + i * 1024, 1024, BF16, "p (h t) -> p h t", h=4) for i in range(2)]
        t_attm = [k.view(f"attm{i}", B0 + 24576 + i * 1024, B0 + 24576 + (i + 1) * 1024) for i in range(2)]
        dec = carve(B0 + 26624, 128, F32, "p (s h) -> p s h", s=NS)
        t_dec = [k.view(f"dec{s}", B0 + 26624 + s * 16, B0 + 26624 + (s + 1) * 16) for s in range(NS)]
        spl = [carve(B0 + 29184 + i * 1024, 1024, BF16) for i in range(2)]
        t_spl = [k.view(f"spl{i}", B0 + 29184 + i * 1024, B0 + 29184 + (i + 1) * 1024) for i in range(2)]
        otmp = [carve(B0 + 27136 + i * 1024, 1024, F32) for i in range(2)]
        t_otmp = [k.view(f"otmp{i}", B0 + 27136 + i * 1024, B0 + 27136 + (i + 1) * 1024) for i in range(2)]

        phase_norm(n_idx)
        k.op(DVE, lambda: nc.vector.memset(glr[0:17, :], 1.0), writes=[t_glr])

        i = ring_load(gla_w_in_d, [(0, 2 * GLA_DK + 2 * GLA_DV, GLA_R)])
        for hf in range(2):
            b = hf
            for kc in range(KC):
                k.op(PE, lambda: nc.tensor.matmul(bank[b][0:16, :], lhsT=wring[i][:, kc, 0:16],
                                                  rhs=hT[:, kc, hf * 512:(hf + 1) * 512],
                                                  start=(kc == 0), stop=(kc == KC - 1)),
                     reads=[t_wring[i]] + t_hT[hf * 4:(hf + 1) * 4], writes=[t_bank[b]], inc=(kc == KC - 1))
            k.op(DVE, lambda: nc.vector.tensor_copy(out=glr[0:16, hf * 512:(hf + 1) * 512], in_=bank[b][0:16, :]),
                 reads=[t_bank[b]], writes=[t_glr, t_bank[b]])
        def gate_A(s):
            bz = 2 + s % 2
            q = s % 2
            k.op(PE, lambda: nc.tensor.matmul(bank[bz][:], lhsT=glr[0:17, s * 128:(s + 1) * 128], rhs=wg2b[0:17, :],
                                              start=True, stop=True),
                 reads=[t_glr, t_wg2b], writes=[t_bank[bz]])
            k.op(ACT, lambda: nc.scalar.activation(out=spb[q][:], in_=bank[bz][:], func=AF.Exp, scale=-1.0),
                 reads=[t_bank[bz]], writes=[t_sp[q], t_bank[bz]])
            k.op(ACT, lambda: nc.scalar.activation(out=spb[q][:], in_=spb[q][:], func=AF.Ln, bias=epsc[:, 1:2]),
                 reads=[t_sp[q], t_epsc], writes=[t_sp[q]])

        def gate_B(s):
            q = s % 2
            bb = 4 + s % 2
            bT = bank[bb][:].rearrange("p (h t) -> p h t", h=4)
            k.op(DVE, lambda: nc.vector.tensor_copy(out=sph[q][:], in_=spb[q][:]),
                 reads=[t_sp[q]], writes=[t_sph[q]])
            k.op(DVE, lambda: nc.vector.tensor_tensor(out=spl[q][:], in0=spb[q][:], in1=sph[q][:],
                                                      op=ALU.subtract),
                 reads=[t_sp[q], t_sph[q]], writes=[t_spl[q]])
            for h in range(4):
                k.op(PE, lambda: nc.tensor.matmul(bT[:, h, :], lhsT=sph[q][:, h * 128:(h + 1) * 128], rhs=trib[:],
                                                  start=True, stop=False),
                     reads=[t_sph[q], t_trib], writes=[t_bank[bb]], inc=False)
                k.op(PE, lambda: nc.tensor.matmul(bT[:, h, :], lhsT=spl[q][:, h * 128:(h + 1) * 128], rhs=trib[:],
                                                  start=False, stop=True),
                     reads=[t_spl[q], t_trib], writes=[t_bank[bb]], inc=(h == 3))
            k.op(ACT, lambda: nc.scalar.activation(out=Eb[:, :, s * 128:(s + 1) * 128], in_=bT, func=AF.Exp),
                 reads=[t_bank[bb]], writes=[t_Eb[s], t_bank[bb]])
            k.op(ACT, lambda: nc.scalar.activation(out=Ei[:, :, s * 128:(s + 1) * 128], in_=bT, func=AF.Exp,
                                                   scale=-1.0),
                 reads=[t_bank[bb]], writes=[t_Ei[s], t_bank[bb]])
            k.op(DVE, lambda: nc.vector.tensor_copy(out=dec[:, s, :], in_=Eb[:, :, s * 128 + 127]),
                 reads=[t_Eb[s]], writes=[t_dec[s]])

        gate_A(0)
        for s in range(NS):
            if s + 1 < NS:
                gate_A(s + 1)
            gate_B(s)

        def q_cons(c, hf, b):
            k.op(DVE, lambda: nc.vector.scalar_tensor_tensor(out=qd[:, c, hf * 512:(hf + 1) * 512], in0=bank[b][:],
                                                             scalar=128.0 ** -0.5,
                                                             in1=Eb[:, c, hf * 512:(hf + 1) * 512],
                                                             op0=ALU.mult, op1=ALU.mult),
                 reads=[t_bank[b]] + t_Eb[hf * 4:(hf + 1) * 4], writes=[t_qd[c][hf], t_bank[b]])

        def k_cons(c, hf, b):
            k.op(DVE, lambda: nc.vector.tensor_tensor(out=ki[:, c, hf * 512:(hf + 1) * 512], in0=bank[b][:],
                                                      in1=Ei[:, c, hf * 512:(hf + 1) * 512], op=ALU.mult),
                 reads=[t_bank[b]] + t_Ei[hf * 4:(hf + 1) * 4], writes=[t_ki[c][hf], t_bank[b]])

        proj_featmajor(gla_w_in_d, 0, 4, q_cons)
        proj_featmajor(gla_w_in_d, GLA_DK, 4, k_cons)
        for s in range(NS):
            i_t = 6 + cnt["t"] % 2; cnt["t"] += 1
            pt = bank_bf(i_t, "p (a b) -> p a b", a=KC)
            for h in range(4):
                k.op(PE, lambda: nc.tensor.transpose(pt[:, h, :], ki[:, h, s * 128:(s + 1) * 128], ident[:]),
                     reads=[t_ki[h][s // 4], t_ident], writes=[t_bank[i_t]], inc=(h == 3))
            k.op(ACT, lambda: nc.scalar.copy(out=kt[:, s, :].rearrange("p (h d) -> p h d", h=4), in_=pt[:, 0:4, :]),
                 reads=[t_bank[i_t]], writes=[t_kt[s], t_bank[i_t]])
        vv = carve(0, 16384, BF16, "p (s c) -> p s c", s=NS)
        t_vv = [k.view(f"vv{s}", s * 2048, (s + 1) * 2048) for s in range(NS)]
        sr = carve(16384, 16384, BF16, "p (s c) -> p s c", s=NS)
        t_sr = [k.view(f"sr{s}", 16384 + s * 2048, 16384 + (s + 1) * 2048) for s in range(NS)]

        def v_cons(p, s, b):
            k.op(ACT, lambda: nc.scalar.copy(out=vv[:, s, p * 256:(p + 1) * 256], in_=bank[b][:, 0:256]),
                 reads=[t_bank[b]], writes=[t_vv[s], t_bank[b]])

        def r_cons(p, s, b):
            q = cnt["sg"] % 2; cnt["sg"] += 1
            k.op(ACT, lambda: nc.scalar.activation(out=sg[q][:, 0:256], in_=bank[b][:, 0:256], func=AF.Silu),
                 reads=[t_bank[b]], writes=[t_sg[q], t_bank[b]])
            k.op(DVE, lambda: nc.vector.tensor_tensor(out=sr[:, s, p * 256:(p + 1) * 256], in0=sg[q][:, 0:256],
                                                      in1=ggout[:], op=ALU.mult),
                 reads=[t_sg[q], t_ggout], writes=[t_sr[s]])

        proj_tokmajor(gla_w_in_d, 2 * GLA_DK, 4, v_cons)
        proj_tokmajor(gla_w_in_d, 2 * GLA_DK + GLA_DV, 4, r_cons)
        wo, t_wo = wout_load(gla_w_out_d)

        at4 = bank[0].rearrange("p (h t) -> p h t", h=4)
        pb = [bank[4].rearrange("p (h v) -> p h v", h=2), bank[5].rearrange("p (h v) -> p h v", h=2)]

        def obank(s, h):
            bi = (2 if s % 2 == 0 else 6) + h // 2
            return bank[bi].rearrange("p (h v) -> p h v", h=2)[:, h % 2, :], t_bank[bi]

        def S1(s):
            f = s // 4
            sl = slice(s * 128, (s + 1) * 128)
            am = s % 2
            for h in range(4):
                k.op(PE, lambda: nc.tensor.matmul(at4[:, h, :], lhsT=ki[:, h, sl], rhs=qd[:, h, sl],
                                                  start=True, stop=True),
                     reads=[t_ki[h][f], t_qd[h][f]], writes=[t_bank[0]], inc=(h == 3))
            k.op(DVE, lambda: nc.vector.tensor_tensor(out=attm[am][:], in0=at4,
                                                      in1=maskT[:].unsqueeze(1).to_broadcast([128, 4, 128]),
                                                      op=ALU.mult),
                 reads=[t_bank[0], t_maskT], writes=[t_attm[am], t_bank[0]])

        def S2(s):
            f = s // 4
            sl = slice(s * 128, (s + 1) * 128)
            am = s % 2
            for h in range(4):
                k.op(PE, lambda: nc.tensor.matmul(pb[h // 2][:, h % 2, :], lhsT=kt[:, s, h * 128:(h + 1) * 128],
                                                  rhs=vv[:, s, h * 256:(h + 1) * 256],
                                                  start=True, stop=True),
                     reads=[t_kt[s], t_vv[s]], writes=[t_bank[4 + h // 2]], inc=(h % 2 == 1))
            for h in range(4):
                o_ap, t_o = obank(s, h)
                k.op(PE, lambda: nc.tensor.matmul(o_ap, lhsT=attm[am][:, h, :],
                                                  rhs=vv[:, s, h * 256:(h + 1) * 256],
                                                  start=True, stop=False),
                     reads=[t_attm[am], t_vv[s]], writes=[t_o], inc=False)
                k.op(PE, lambda: nc.tensor.matmul(o_ap, lhsT=qd[:, h, sl], rhs=Sbf[:, h, :],
                                                  start=False, stop=True),
                     reads=[t_qd[h][f], t_Sbf[h]], writes=[t_o], inc=(h % 2 == 1))
            for h in range(4):
                k.op(DVE, lambda: nc.vector.tensor_tensor(out=Sst[:, h, :], in0=pb[h // 2][:, h % 2, :],
                                                          in1=Sst[:, h, :], op=ALU.add),
                     reads=[t_bank[4 + h // 2], t_S[h]], writes=[t_S[h], t_bank[4 + h // 2]])
                k.op(DVE, lambda: nc.vector.tensor_scalar(out=Sst[:, h, :], in0=Sst[:, h, :],
                                                          scalar1=dec[:, s, h:h + 1], scalar2=None, op0=ALU.mult),
                     reads=[t_S[h], t_dec[s]], writes=[t_S[h]])
                k.op(POOL, lambda: nc.gpsimd.tensor_copy(out=Sbf[:, h, :], in_=Sst[:, h, :]),
                     reads=[t_S[h]], writes=[t_Sbf[h]])

        def S3(s):
            io = s % 2
            for h in range(4):
                o_ap, t_o = obank(s, h)
                k.op(ACT, lambda: nc.scalar.activation(out=junk[:, 0:256], in_=o_ap,
                                                       func=AF.Square, accum_out=small[:, h:h + 1]),
                     reads=[t_o], writes=[t_junk, t_small, t_o])
            k.op(ACT, lambda: nc.scalar.activation(out=small[:, 4:8], in_=small[:, 0:4], func=AF.Ln,
                                                   bias=epsc[:, 0:1], scale=1.0 / 256),
                 reads=[t_small, t_epsc], writes=[t_small])
            k.op(ACT, lambda: nc.scalar.activation(out=small[:, 8:12], in_=small[:, 4:8], func=AF.Exp, scale=-0.5),
                 reads=[t_small], writes=[t_small])
            for h in range(4):
                o_ap, t_o = obank(s, h)
                k.op(DVE, lambda: nc.vector.scalar_tensor_tensor(out=og[io][:, h * 256:(h + 1) * 256], in0=o_ap,
                                                                 scalar=small[:, 8 + h:9 + h],
                                                                 in1=sr[:, s, h * 256:(h + 1) * 256],
                                                                 op0=ALU.mult, op1=ALU.mult),
                     reads=[t_o, t_small, t_sr[s]], writes=[t_og[io], t_o])

        def S4(s):
            io = s % 2
            transpose_to_hT(lambda kc: og[io][:, kc * 128:(kc + 1) * 128], t_og[io], s, use_bank=1)

        S1(0)
        for s in range(NS):
            if s + 1 < NS:
                S1(s + 1)
            S2(s)
            if s >= 1:
                S3(s - 1)
            if s >= 2:
                S4(s - 2)
        S3(NS - 1)
        S4(NS - 2)
        S4(NS - 1)
        out_proj_fused(wo, t_wo, nxt)

    kst_sem = [k.dsem(f"kst{i}") for i in range(2)]
    vst_sem = [k.dsem(f"vst{i}") for i in range(5)]
    kc_list = [[] for _ in range(8)]
    vc_list = [[] for _ in range(8)]
    ld_sems = [k.dsem(f"kld{i}") for i in range(2)] + [k.dsem(f"vld{i}") for i in range(2)]

    def diff(n_idx, ti, nxt):
        B0 = A_BYTES + W_BYTES
        qT = carve(0, 16384, BF16, "p (h t) -> p h t", h=8)
        t_qT = [[k.view(f"qT{h}_{f}", (h * TT + f * 512) * 2, (h * TT + f * 512 + 512) * 2) for f in range(2)]
                for h in range(8)]
        kTh = [carve(16384 + i * 8192, 8192, BF16) for i in range(2)]
        t_kTh = [k.view(f"kTh{i}", 16384 + i * 8192, 16384 + (i + 1) * 8192, dsem=ld_sems[i]) for i in range(2)]
        kst = [carve(32768 + i * 2048, 2048, BF16) for i in range(2)]
        t_kst = [k.view(f"kst{i}", 32768 + i * 2048, 32768 + (i + 1) * 2048) for i in range(2)]
        vst = [carve(36864 + i * 520, 520, BF16, "p (h c) -> p h c", h=2) for i in range(5)]
        t_vst = [k.view(f"vst{i}", 36864 + i * 520, 36864 + (i + 1) * 520) for i in range(5)]
        osb = carve(39936, 4160, F32, "p (j m c) -> p j m c", j=4, m=2)
        t_osb = [k.view(f"osb{j}", 39936 + j * 1040, 39936 + (j + 1) * 1040) for j in range(4)]
        pT = [carve(B0 + i * 2048, 2048, BF16, "p (m q) -> p m q", m=2) for i in range(3)]
        t_pT = [[k.view(f"pT{i}_{j}", B0 + i * 2048 + j * 512, B0 + i * 2048 + (j + 1) * 512) for j in range(4)]
                for i in range(3)]
        VB = 8320
        Vh = [carve(B0 + 6144 + i * VB, VB, BF16, "p (b c) -> p b c", c=130) for i in range(2)]
        t_Vh = [k.view(f"Vh{i}", B0 + 6144 + i * VB, B0 + 6144 + (i + 1) * VB, dsem=ld_sems[2 + i])
                for i in range(2)]
        ogo = [B0 + 6144 + 2 * VB, A_BYTES + W_BYTES - 8192]
        og = [carve(o_, 8192, BF16, "p (j c) -> p j c", j=4) for o_ in ogo]
        t_og = [[k.view(f"dog{gi}_{j}", ogo[gi] + j * 2048, ogo[gi] + (j + 1) * 2048) for j in range(4)]
                for gi in range(2)]

        phase_norm(n_idx)
        for i in range(5):
            k.op(DVE, lambda: nc.vector.memset(vst[i][:, :, 128:130], 1.0), writes=[t_vst[i]])

        def q_cons(c, hf, b):
            k.op(ACT, lambda: nc.scalar.copy(out=qT[:, c, hf * 512:(hf + 1) * 512], in_=bank[b][:]),
                 reads=[t_bank[b]], writes=[t_qT[c][hf], t_bank[b]])

        def k_cons(c, hf, b):
            i = c % 2
            k.op(DVE, lambda: nc.vector.tensor_copy(out=kst[i][:, hf * 512:(hf + 1) * 512], in_=bank[b][:]),
                 reads=[t_bank[b]], writes=[t_kst[i], t_bank[b]])
            if hf == 1:
                tk = Tl(f"kc{c}_{ti}", kst_sem[i]); kc_list[c].append(tk)
                k.dma(SP, tk, t_kst[i], [(kcache[c, :, ti * TT:(ti + 1) * TT], kst[i][:])])

        def v_cons(p, s, b):
            i = cnt["sg"] % 5; cnt["sg"] += 1
            k.op(ACT, lambda: nc.scalar.copy(out=vst[i][:, :, 0:128],
                                             in_=bank[b][:, 0:256].rearrange("p (h c) -> p h c", h=2)),
                 reads=[t_bank[b]], writes=[t_vst[i], t_bank[b]])
            blk = ti * NS + s
            tv = Tl(f"vc{p}_{blk}", vst_sem[i])
            vc_list[2 * p].append(tv); vc_list[2 * p + 1].append(tv)
            k.dma(SP, tv, t_vst[i], [(vcache[2 * p + hh, :, blk, :], vst[i][:, hh, :]) for hh in range(2)])

        proj_featmajor(diff_w_in_d, 0, 8, q_cons)
        proj_featmajor(diff_w_in_d, D, 8, k_cons)
        proj_tokmajor(diff_w_in_d, 2 * D, 4, v_cons)
        wo, t_wo = wout_load(diff_w_out_d)

        obk = [bank[4 + j].rearrange("p (m c) -> p m c", m=2) for j in range(4)]
        units = [(g, h) for g in range(2) for h in range(8)]
        itc = [0]
        itof = {}

        def geom(g):
            nk = ti * TT + (g + 1) * 512
            return nk, nk // 128, (ti * TT + g * 512) // 128

        def loads(ui):
            g, h = units[ui]
            li = ui % 2
            nk, nkb, qb0 = geom(g)
            k.dma(SP, t_kTh[li], list(kc_list[h]), [(kTh[li][:, 0:nk], kcache[h, :, 0:nk])])
            k.dma(SP, t_Vh[li], list(vc_list[h]), [(Vh[li][:, 0:nkb, :], vcache[h, :, 0:nkb, :])])

        def qk_exp(ui, kb):
            g, h = units[ui]
            li = ui % 2
            nk, nkb, qb0 = geom(g)
            itn = itc[0]; itc[0] += 1
            itof[(ui, kb)] = itn
            j0 = max(0, kb - qb0)
            b0 = 2 * (itn % 2)
            pi = itn % 3
            c0 = j0 * 128
            for m in range(2):
                bm = b0 + m
                lhs = kTh[li][m * 64:(m + 1) * 64, kb * 128:(kb + 1) * 128]
                k.op(PE, lambda: nc.tensor.matmul(bank[bm][:, c0:512], lhsT=lhs,
                                                  rhs=qT[m * 64:(m + 1) * 64, h, g * 512 + c0:(g + 1) * 512],
                                                  start=True, stop=True),
                     reads=[t_kTh[li], t_qT[h][g]], writes=[t_bank[bm]])
            k.op(ACT, lambda: nc.scalar.activation(out=pT[pi][:, :, c0:512],
                                                   in_=psall[:, b0:b0 + 2, c0:512], func=AF.Exp,
                                                   scale=0.125),
                 reads=[t_bank[b0], t_bank[b0 + 1]], writes=t_pT[pi][j0:4] + [t_bank[b0], t_bank[b0 + 1]])
            if kb >= qb0:
                k.op(DVE, lambda: nc.vector.tensor_tensor(out=pT[pi][:, :, c0:c0 + 128],
                                                          in0=pT[pi][:, :, c0:c0 + 128],
                                                          in1=maskTb[:].unsqueeze(1).to_broadcast([128, 2, 128]),
                                                          op=ALU.mult),
                     reads=[t_pT[pi][j0], t_maskTb], writes=[t_pT[pi][j0]])

        def pv(ui, kb):
            g, h = units[ui]
            li = ui % 2
            nk, nkb, qb0 = geom(g)
            j0 = max(0, kb - qb0)
            pi = itof[(ui, kb)] % 3
            if kb == 0:
                for j in range(4):
                    k.op(PE, lambda: nc.tensor.matmul(bank[4 + j][:, :], lhsT=zer[:, 0:128], rhs=zer[:, :],
                                                      start=True, stop=False),
                         reads=[t_zer], writes=[t_bank[4 + j]], inc=False)
            for j in range(3, j0 - 1, -1):
                last = (kb == qb0 + j)
                for m in range(2):
                    k.op(PE, lambda: nc.tensor.matmul(obk[j][:, m, 0:129],
                                                      lhsT=pT[pi][:, m, j * 128:(j + 1) * 128],
                                                      rhs=Vh[li][:, kb, 0:129],
                                                      start=False, stop=(last and m == 1)),
                         reads=[t_pT[pi][j], t_Vh[li]], writes=[t_bank[4 + j]], inc=(m == 1))

        def evac():
            for j in range(4):
                tb = t_bank[4 + j]
                k.op(DVE, lambda: nc.vector.tensor_copy(out=osb[:, j, :, 0:129], in_=obk[j][:, :, 0:129]),
                     reads=[tb], writes=[t_osb[j], tb])

        def finalize_a(ui):
            for j in range(4):
                to = t_osb[j]
                k.op(DVE, lambda: nc.vector.reciprocal(out=small[:, 16 + 2 * j:18 + 2 * j],
                                                       in_=osb[:, j, :, 128]),
                     reads=[to], writes=[t_small])
                k.op(DVE, lambda: nc.vector.tensor_tensor(out=small[:, 17 + 2 * j:18 + 2 * j],
                                                          in0=small[:, 17 + 2 * j:18 + 2 * j],
                                                          in1=lams[:, 4:5], op=ALU.mult),
                     reads=[t_small, t_lams], writes=[t_small])
                k.op(DVE, lambda: nc.vector.tensor_scalar(out=osb[:, j, 1, 0:128], in0=osb[:, j, 1, 0:128],
                                                          scalar1=small[:, 17 + 2 * j:18 + 2 * j], scalar2=None,
                                                          op0=ALU.mult),
                     reads=[to, t_small], writes=[to])
                k.op(DVE, lambda: nc.vector.scalar_tensor_tensor(out=osb[:, j, 0, 0:128],
                                                                 in0=osb[:, j, 0, 0:128],
                                                                 scalar=small[:, 16 + 2 * j:17 + 2 * j],
                                                                 in1=osb[:, j, 1, 0:128],
                                                                 op0=ALU.mult, op1=ALU.add),
                     reads=[to, t_small], writes=[to])

        def finalize_b(ui):
            g, h = units[ui]
            for j in range(4):
                to = t_osb[j]
                k.op(ACT, lambda: nc.scalar.activation(out=junk[:, 0:128], in_=osb[:, j, 0, 0:128],
                                                       func=AF.Square, accum_out=small[:, 32 + j:33 + j]),
                     reads=[to], writes=[t_junk, t_small])
            k.op(ACT, lambda: nc.scalar.activation(out=small[:, 36:40], in_=small[:, 32:36], func=AF.Ln,
                                                   bias=epsc[:, 0:1], scale=1.0 / 128),
                 reads=[t_small, t_epsc], writes=[t_small])
            k.op(ACT, lambda: nc.scalar.activation(out=small[:, 40:44], in_=small[:, 36:40], func=AF.Exp,
                                                   scale=-0.5),
                 reads=[t_small], writes=[t_small])
            for j in range(4):
                k.op(DVE, lambda: nc.vector.scalar_tensor_tensor(out=og[g % 2][:, j, h * 128:(h + 1) * 128],
                                                                 in0=osb[:, j, 0, 0:128],
                                                                 scalar=small[:, 40 + j:41 + j], in1=dgout[:],
                                                                 op0=ALU.mult, op1=ALU.mult),
                     reads=[t_osb[j], t_small, t_dgout], writes=[t_og[g % 2][j]])

        def group_transposes(g):
            for j in range(4):
                transpose_to_hT(lambda kc: og[g % 2][:, j, kc * 128:(kc + 1) * 128], t_og[g % 2][j], g * 4 + j,
                                use_bank=(j if g == 0 else None))

        loads(0)
        qk_exp(0, 0)
        for ui in range(len(units)):
            g, h = units[ui]
            nk, nkb, qb0 = geom(g)
            if ui + 1 < len(units):
                loads(ui + 1)
            for kb in range(nkb):
                if kb + 1 < nkb:
                    qk_exp(ui, kb + 1)
                pv(ui, kb)
                if ui > 0 and kb == 2:
                    finalize_b(ui - 1)
                if g == 1 and h == 0 and kb == 3:
                    group_transposes(0)
            evac()
            if ui + 1 < len(units):
                qk_exp(ui + 1, 0)
            finalize_a(ui)
        finalize_b(len(units) - 1)
        group_transposes(1)
        out_proj_fused(wo, t_wo, nxt)

    def final_store(tile_i, do_norm):
        outs = []
        if do_norm:
            gf = carve(0, 4096, F32); t_gf = k.view("gfin", 0, 4096, dsem=gf_sem)
            ob = [carve(4096 + i * 4096, 4096, F32) for i in range(2)]
            k.dma(SP, t_gf, None, [(gf[:], gfin_d)])
            if pre_done["v"]:
                pre_done["v"] = False
            else:
                rms_stats()
        for s in range(NS):
            r0 = tile_i * TT + s * 128
            if do_norm:
                i = s % 2
                t_ob = k.view(f"ob{i}", 4096 + i * 4096, 8192 + i * 4096, dsem=None)
                k.op(DVE, lambda: nc.vector.scalar_tensor_tensor(out=ob[i][:], in0=xt[:, s, :],
                                                                 scalar=rstd[:, s:s + 1], in1=gf[:],
                                                                 op0=ALU.mult, op1=ALU.mult),
                     reads=[t_xt[s], t_rstd[s], t_gf], writes=[t_ob])
                t_out = Tl(f"o{tile_i}_{s}", out_sems[i])
                k.dma(SP, t_out, t_ob, [(out_d[r0:r0 + 128, :], ob[i][:])])
            else:
                t_out = Tl(f"o{tile_i}_{s}", out_sems[s % 2])
                k.dma(SP, t_out, t_xt[s], [(out_d[r0:r0 + 128, :], xt[:, s, :])])
            outs.append(t_out)
        return outs

    for ti in range(ntiles):
        if ti == 0 or stages < 7:
            load_x(ti)
        if stages >= 1:
            ffn(0, 0, nxt=1 if stages >= 2 else None)
        if stages >= 2:
            gla(1, 2)
        if stages >= 3:
            ffn(1, 2, nxt=3 if stages >= 4 else None)
        if stages >= 4:
            ffn(2, 3, nxt=4 if stages >= 5 else None)
        if stages >= 5:
            diff(4, ti, 5)
        if stages >= 6:
            ffn(3, 5, nxt="final" if stages >= 7 else None, ti=ti, has_next=(ti + 1 < ntiles))
        if stages < 7:
            pre_done["v"] = False
            last_out += final_store(ti, False)
    for t in last_out[-4:]:
        k.wait_tile(SP, t)
    for key in out_sems:
        SP.h.wait_ge(k.semof[key], k.dcnt[key])
    print(f"[build] inst={k.ninst} waits={k.nwaits} sems={k.nsem}")
    stack.close()
    return nc


def _consts():
    c = {}
    c["ident"] = np.eye(128, dtype=np.float32).astype(ml_dtypes.bfloat16)
    s = np.arange(128)[:, None]
    t = np.arange(128)[None, :]
    c["tri"] = np.where(s <= t, -1.0 / 16.0, 0.0).astype(np.float32)
    c["maskT"] = np.where(s <= t, 1.0, 0.0).astype(np.float32)
    c["negmask"] = np.where(s > t, NEG, 0.0).astype(np.float32).astype(ml_dtypes.bfloat16)
    return c


def make_in_maps(inputs, n_cores=8):
    f = lambda n: np.ascontiguousarray(np.asarray(inputs[n], np.float32))
    x = f("x")
    shared = dict(_consts())
    norm_names = ["l0_norm_ffn1", "l0_norm_mix", "l0_norm_ffn2", "l1_norm_ffn1", "l1_norm_mix", "l1_norm_ffn2",
                  "final_norm"]
    norms = np.stack([f(n).reshape(KC, 128).T for n in norm_names], axis=1)
    shared["norms"] = np.ascontiguousarray(norms)
    ffn_names = [("l0_ffn1_w_in", "l0_ffn1_w_down"), ("l0_ffn2_w_in", "l0_ffn2_w_down"),
                 ("l1_ffn1_w_in", "l1_ffn1_w_down"), ("l1_ffn2_w_in", "l1_ffn2_w_down")]
    for i, (a, b) in enumerate(ffn_names):
        shared[f"ffn{i}_w_in"] = f(a)
        shared[f"ffn{i}_w_down"] = f(b)
    shared["gla_w_in"] = f("l0_gla_w_in")
    shared["gla_wg2b"] = np.ascontiguousarray(np.concatenate([f("l0_gla_w_gate2"), f("l0_gla_b_gate2")[None, :]], 0))
    shared["gla_gout"] = np.ascontiguousarray(np.broadcast_to(f("l0_gla_norm_out")[None, :], (128, 256)))
    shared["gla_w_out"] = f("l0_gla_w_out")
    shared["diff_w_in"] = f("l1_diff_w_in")
    lam = np.stack([f("l1_diff_lambda_q1"), f("l1_diff_lambda_k1"), f("l1_diff_lambda_q2"), f("l1_diff_lambda_k2")])
    shared["diff_lam"] = np.ascontiguousarray(np.broadcast_to(lam[None], (128, 4, 64)))
    shared["diff_gout"] = np.ascontiguousarray(np.broadcast_to(f("l1_diff_norm_out")[None, :], (128, 128)))
    shared["diff_w_out"] = f("l1_diff_w_out")
    shared["gfin"] = np.ascontiguousarray(np.broadcast_to(f("final_norm")[None, :], (128, D)))
    maps = []
    for ci in range(n_cores):
        m = dict(shared)
        m["x"] = np.ascontiguousarray(x[ci])
        maps.append(m)
    return maps


def kernel(**inputs):
    nc = build_nc()
    in_maps = make_in_maps(inputs, 8)
    res = run_bass_kernel_spmd(nc, in_maps, core_ids=list(range(8)))
    return np.stack([np.asarray(r["out"], dtype=np.float32) for r in res.results], axis=0)
```
